# Optimizing a Trainium2 kernel written in Bass

```python
import math
import jax
import jax.numpy as jnp
from jax import lax
import numpy as np

D_MODEL = 1024
BATCH = 8
SEQ = 4096
DEPTH = 2

CHUNK = 128
RET_HEADS = 4
RET_DK = 128
RET_DV = 256
RET_QK_W = RET_HEADS * RET_DK
RET_V_W = RET_HEADS * RET_DV
DN_HEADS = 8
DN_DK = 128
DN_DV = 128
DN_QK_W = DN_HEADS * DN_DK
DN_V_W = DN_HEADS * DN_DV
CONV_K = 5
D_FF = 4 * D_MODEL
N_BRANCH = 2
ROPE_THETA = 10000.0
EPS = 1e-6
COL_SIZES = (RET_QK_W, RET_QK_W, RET_V_W, RET_V_W,
             DN_QK_W, DN_QK_W, DN_V_W, DN_V_W,
             2 * DN_HEADS, 2 * DN_HEADS, N_BRANCH * D_MODEL)
IN_COLS = 2 * RET_QK_W + 2 * RET_V_W + 2 * DN_QK_W + 2 * DN_V_W + 4 * DN_HEADS + N_BRANCH * D_MODEL

kernel_name = 'hybrid_retention_gated_deltanet_encoder'


def rmsnorm(x, g):
    xf = x.astype(jnp.float32)
    y = xf * lax.rsqrt(jnp.mean(xf * xf, axis=-1, keepdims=True) + EPS)
    return (y * g.astype(jnp.float32)).astype(x.dtype)


def split_cols(u):
    out, off = [], 0
    for n in COL_SIZES:
        out.append(u[..., off:off + n])
        off += n
    return out


def rotary(t, pos):
    half = t.shape[-1] // 2
    inv = ROPE_THETA ** (-jnp.arange(half, dtype=jnp.float32) / half)
    ang = pos.astype(jnp.float32)[:, None] * inv[None, :]
    cos = jnp.cos(ang)[None, :, None, :]
    sin = jnp.sin(ang)[None, :, None, :]
    t1, t2 = t[..., :half], t[..., half:]
    return jnp.concatenate([t1 * cos - t2 * sin, t1 * sin + t2 * cos], axis=-1)


def to_chunks(t):
    B, S, H = t.shape[:3]
    n = S // CHUNK
    if t.ndim == 4:
        return t.reshape(B, n, CHUNK, H, t.shape[-1]).transpose(0, 3, 1, 2, 4)
    return t.reshape(B, n, CHUNK, H).transpose(0, 3, 1, 2)


def from_chunks(t):
    B, H, n, C, D = t.shape
    return t.transpose(0, 2, 3, 1, 4).reshape(B, n * C, H, D)


def retention_bidir(q, k, v):
    B, S, H, Dk = q.shape
    Dv = v.shape[-1]
    gamma = 1.0 - 2.0 ** (-5.0 - jnp.arange(H, dtype=jnp.float32))
    log_g = jnp.log(gamma)
    idx = jnp.arange(CHUNK, dtype=jnp.float32)
    qc, kc, vc = to_chunks(q), to_chunks(k), to_chunks(v)
    intra = jnp.exp(log_g[:, None, None] * jnp.abs(idx[:, None] - idx[None, :]))
    scores = jnp.einsum('bhnid,bhnjd->bhnij', qc, kc) * intra[None, :, None]
    o = jnp.einsum('bhnij,bhnjv->bhniv', scores, vc)
    k_f = kc * jnp.exp(log_g[:, None] * (CHUNK - 1.0 - idx))[None, :, None, :, None]
    k_b = kc * jnp.exp(log_g[:, None] * idx)[None, :, None, :, None]
    kv_f = jnp.einsum('bhnjd,bhnjv->nbhdv', k_f, vc)
    kv_b = jnp.einsum('bhnjd,bhnjv->nbhdv', k_b, vc)
    chunk_dec = jnp.exp(log_g * CHUNK)[None, :, None, None]

    def step(state, kv):
        return chunk_dec * state + kv, state

    init = jnp.zeros((B, H, Dk, Dv), jnp.float32)
    _, s_f = lax.scan(step, init, kv_f)
    _, s_b = lax.scan(step, init, kv_b, reverse=True)
    q_f = qc * jnp.exp(log_g[:, None] * (idx + 1.0))[None, :, None, :, None]
    q_b = qc * jnp.exp(log_g[:, None] * (CHUNK - idx))[None, :, None, :, None]
    o = o + jnp.einsum('bhnid,nbhdv->bhniv', q_f, s_f) + jnp.einsum('bhnid,nbhdv->bhniv', q_b, s_b)
    return from_chunks(o)


def gated_delta_chunked(q, k, v, beta, g):
    B, S, H, Dk = q.shape
    Dv = v.shape[-1]
    qc, kc, vc = to_chunks(q), to_chunks(k), to_chunks(v)
    bc = to_chunks(beta)
    gc = jnp.cumsum(to_chunks(g), axis=-1)
    tri_strict = jnp.tril(jnp.ones((CHUNK, CHUNK), bool), -1)
    tri_incl = jnp.tril(jnp.ones((CHUNK, CHUNK), bool))
    diff = gc[..., :, None] - gc[..., None, :]
    dec_strict = jnp.exp(jnp.where(tri_strict, diff, -jnp.inf))
    dec_incl = jnp.exp(jnp.where(tri_incl, diff, -jnp.inf))
    a_mat = bc[..., :, None] * jnp.einsum('bhnid,bhnjd->bhnij', kc, kc) * dec_strict
    lhs = a_mat + jnp.eye(CHUNK, dtype=jnp.float32)
    rhs = jnp.concatenate([bc[..., None] * vc, (bc * jnp.exp(gc))[..., None] * kc], axis=-1)
    sol = lax.linalg.triangular_solve(lhs, rhs, left_side=True, lower=True, unit_diagonal=True)
    u, w = sol[..., :Dv], sol[..., Dv:]
    attn = jnp.einsum('bhnid,bhnjd->bhnij', qc, kc) * dec_incl
    q_dec = qc * jnp.exp(gc)[..., None]
    k_dec = kc * jnp.exp(gc[..., -1:] - gc)[..., None]
    chunk_dec = jnp.exp(gc[..., -1])

    def step(state, xs):
        u_n, w_n, attn_n, qd_n, kd_n, cd_n = xs
        delta = u_n - jnp.einsum('bhcd,bhdv->bhcv', w_n, state)
        o_n = jnp.einsum('bhcd,bhdv->bhcv', qd_n, state) + jnp.einsum('bhij,bhjv->bhiv', attn_n, delta)
        state = cd_n[..., None, None] * state + jnp.einsum('bhcd,bhcv->bhdv', kd_n, delta)
        return state, o_n

    xs = tuple(jnp.moveaxis(t, 2, 0) for t in (u, w, attn, q_dec, k_dec, chunk_dec))
    init = jnp.zeros((B, H, Dk, Dv), jnp.float32)
    _, o = lax.scan(step, init, xs)
    return from_chunks(jnp.moveaxis(o, 0, 2))


def gated_delta_bidir(q, k, v, beta_f, beta_b, g_f, g_b):
    fwd = gated_delta_chunked(q, k, v, beta_f, g_f)
    fl = lambda t: jnp.flip(t, axis=1)
    bwd = fl(gated_delta_chunked(fl(q), fl(k), fl(v), fl(beta_b), fl(g_b)))
    return fwd + bwd


def centred_dwconv(x, w):
    K = w.shape[0]
    return lax.conv_general_dilated(
        x, w[:, None, :].astype(x.dtype), window_strides=(1,),
        padding=[((K - 1) // 2, K // 2)],
        dimension_numbers=('NWC', 'WIO', 'NWC'),
        feature_group_count=x.shape[-1])


def l2norm(t):
    return t * lax.rsqrt(jnp.sum(t * t, axis=-1, keepdims=True) + EPS)


def setup_inputs(seed: int = 0) -> dict:
    key = jax.random.key(seed)
    ks = jax.random.split(key, 16)
    f32 = jnp.float32

    def dense(k, shape, fan_in):
        return jax.random.normal(k, shape, f32) * fan_in ** -0.5

    def gain(k, shape):
        return 1.0 + 0.02 * jax.random.normal(k, shape, f32)

    x = jax.random.normal(ks[0], (BATCH, SEQ, D_MODEL), f32)
    norm1_g = gain(ks[1], (DEPTH, D_MODEL))
    w_in = dense(ks[2], (DEPTH, D_MODEL, IN_COLS), D_MODEL)
    conv_w = dense(ks[3], (DEPTH, CONV_K, 2 * DN_QK_W + DN_V_W), CONV_K)
    a_log = jnp.log(jax.random.uniform(ks[4], (DEPTH, 2, DN_HEADS), f32, 1.0, 16.0))
    dt = jnp.exp(jax.random.uniform(ks[5], (DEPTH, 2, DN_HEADS), f32, math.log(1e-3), math.log(1e-1)))
    dt_bias = dt + jnp.log(-jnp.expm1(-dt))
    ret_gn_g = gain(ks[6], (DEPTH, RET_V_W))
    dn_norm_g = gain(ks[7], (DEPTH, DN_DV))
    w_proj_ret = dense(ks[8], (DEPTH, RET_V_W, D_MODEL), RET_V_W)
    w_proj_dn = dense(ks[9], (DEPTH, DN_V_W, D_MODEL), DN_V_W)
    w_out = dense(ks[10], (DEPTH, D_MODEL, D_MODEL), D_MODEL)
    norm2_g = gain(ks[11], (DEPTH, D_MODEL))
    w_ff1 = dense(ks[12], (DEPTH, D_MODEL, D_FF), D_MODEL)
    w_ff2 = dense(ks[13], (DEPTH, D_FF, D_MODEL), D_FF)
    final_g = gain(ks[14], (D_MODEL,))
    return {'x': x, 'norm1_g': norm1_g, 'w_in': w_in, 'conv_w': conv_w, 'a_log': a_log,
            'dt_bias': dt_bias, 'ret_gn_g': ret_gn_g, 'dn_norm_g': dn_norm_g,
            'w_proj_ret': w_proj_ret, 'w_proj_dn': w_proj_dn, 'w_out': w_out,
            'norm2_g': norm2_g, 'w_ff1': w_ff1, 'w_ff2': w_ff2, 'final_g': final_g}


def reference(x, norm1_g, w_in, conv_w, a_log, dt_bias, ret_gn_g, dn_norm_g,
              w_proj_ret, w_proj_dn, w_out, norm2_g, w_ff1, w_ff2, final_g):
    B, S, _ = x.shape
    f32 = jnp.float32
    pos = jnp.arange(S, dtype=jnp.int32)
    for l in range(DEPTH):
        h = rmsnorm(x, norm1_g[l])
        (r_q, r_k, r_v, r_g, d_q, d_k, d_v, d_z, d_b, d_a, gate_raw) = split_cols(h @ w_in[l])

        q = rotary(r_q.astype(f32).reshape(B, S, RET_HEADS, RET_DK), pos) * RET_DK ** -0.5
        k = rotary(r_k.astype(f32).reshape(B, S, RET_HEADS, RET_DK), pos)
        v = r_v.astype(f32).reshape(B, S, RET_HEADS, RET_DV)
        o = retention_bidir(q, k, v)
        mu = jnp.mean(o, axis=-1, keepdims=True)
        var = jnp.mean(jnp.square(o - mu), axis=-1, keepdims=True)
        o = (o - mu) * lax.rsqrt(var + EPS) * ret_gn_g[l].astype(f32).reshape(RET_HEADS, RET_DV)
        y_ret = (o.reshape(B, S, RET_V_W) * jax.nn.silu(r_g.astype(f32))).astype(x.dtype)

        qkv = jax.nn.silu(centred_dwconv(jnp.concatenate([d_q, d_k, d_v], axis=-1), conv_w[l]).astype(f32))
        dq = l2norm(qkv[..., :DN_QK_W].reshape(B, S, DN_HEADS, DN_DK)) * DN_DK ** -0.5
        dk = l2norm(qkv[..., DN_QK_W:2 * DN_QK_W].reshape(B, S, DN_HEADS, DN_DK))
        dv = qkv[..., 2 * DN_QK_W:].reshape(B, S, DN_HEADS, DN_DV)
        beta = jax.nn.sigmoid(d_b.astype(f32)).reshape(B, S, 2, DN_HEADS)
        g = -jnp.exp(a_log[l].astype(f32)) * jax.nn.softplus(
            d_a.astype(f32).reshape(B, S, 2, DN_HEADS) + dt_bias[l].astype(f32))
        od = gated_delta_bidir(dq, dk, dv, beta[:, :, 0], beta[:, :, 1], g[:, :, 0], g[:, :, 1])
        od = od * lax.rsqrt(jnp.mean(od * od, axis=-1, keepdims=True) + EPS) * dn_norm_g[l].astype(f32)
        y_dn = (od.reshape(B, S, DN_V_W) * jax.nn.silu(d_z.astype(f32))).astype(x.dtype)

        gates = jax.nn.sigmoid(gate_raw.astype(f32)).reshape(B, S, N_BRANCH, D_MODEL).astype(x.dtype)
        merged = gates[:, :, 0] * (y_ret @ w_proj_ret[l]) + gates[:, :, 1] * (y_dn @ w_proj_dn[l])
        x = x + merged @ w_out[l]

        h2 = rmsnorm(x, norm2_g[l])
        x = x + jnp.square(jax.nn.relu(h2 @ w_ff1[l])) @ w_ff2[l]
    return rmsnorm(x, final_g)
```

```python
import numpy as np
from contextlib import ExitStack
import concourse.bass as bass
import concourse.mybir as mybir
from concourse.bass_utils import run_bass_kernel_spmd

F32 = mybir.dt.float32
BF16 = mybir.dt.bfloat16
AF = mybir.ActivationFunctionType
ALU = mybir.AluOpType
AX = mybir.AxisListType

D = 1024
DEPTH = 2
NH_R, DK_R, DV_R = 4, 128, 256
NH_D = 8
DFF = 4096
IN_COLS = 9248
EPS = 1e-6
C_RQ, C_RK, C_RV, C_RG = 0, 512, 1024, 2048
C_DQ, C_DK, C_DV, C_DZ = 3072, 4096, 5120, 6144
C_DB, C_DA, C_GATE = 7168, 7184, 7200
BIG = 30000.0


class _Stop(Exception):
    pass


class Buf:
    __slots__ = ("name", "lw", "rd", "dsem", "psum")

    def __init__(self, name, dsem=None):
        self.name = name
        self.lw = None
        self.rd = []
        self.dsem = dsem
        self.psum = name.startswith("p") and name[1:2].isupper()


class Ctx:
    def __init__(self, nc, es, n_dsem=28):
        self.nc = nc
        self.es = es
        self.eng = {"pe": nc.tensor, "act": nc.scalar, "dve": nc.vector,
                    "pool": nc.gpsimd, "sp": nc.sync}
        self.sem, self.cnt = {}, {}
        for k in ("pe", "act", "dve", "pool"):
            self.sem[k] = es.enter_context(nc.semaphore("sem_" + k))
            self.cnt[k] = 0
        self.dsem, self.dcnt = [], []
        for i in range(n_dsem):
            self.dsem.append(es.enter_context(nc.semaphore("ds%d" % i)))
            self.dcnt.append(0)
        self.next_dsem = 0
        self.waited = {k: {} for k in self.eng}
        self.n_inst = 0
        self.scope = None
        self.uid = 0

    def dram(self, name, shape, dt, kind="Internal"):
        return self.nc.dram_tensor(name, list(shape), dt, kind=kind).ap()

    def open(self):
        self.scope = ExitStack()
        self.next_dsem = 0
        return self.scope

    def sb(self, name, shape, dt):
        self.uid += 1
        return self.scope.enter_context(self.nc.sbuf_tensor("%s_%d" % (name, self.uid), list(shape), dt))

    def ps(self, name, shape, dt):
        self.uid += 1
        return self.scope.enter_context(self.nc.psum_tensor("%s_%d" % (name, self.uid), list(shape), dt))

    def buf(self, name, dma=False):
        if dma:
            i = self.next_dsem
            self.next_dsem += 1
            assert i < len(self.dsem), "out of dma semaphores"
            return Buf(name, i)
        return Buf(name)

    def _wait(self, e, tok):
        kind, key, val = tok
        w = self.waited[e]
        if w.get((kind, key), 0) >= val:
            return
        w[(kind, key)] = val
        sem = self.sem[key] if kind == "e" else self.dsem[key]
        self.eng[e].wait_ge(sem, val)

    def _deps(self, e, r, w):
        toks = []
        raw = set()
        for b in r:
            if b.lw is not None:
                toks.append(b.lw)
                raw.add(b.lw)
        for b in w:
            if b.lw is not None:
                toks.append(b.lw)
            toks.extend(b.rd)
        for tok in toks:
            if tok[0] == "e" and tok[1] == e and tok not in raw:
                continue
            self._wait(e, tok)

    def _commit(self, tok, r, w):
        for b in r:
            b.rd.append(tok)
            if len(b.rd) > 64:
                b.rd = b.rd[-64:]
        for b in w:
            b.lw = tok
            b.rd = []

    def op(self, e, fn, r=(), w=()):
        w = list(w) + [b for b in r if b.psum and b not in w]
        self._deps(e, r, w)
        ins = fn(self.eng[e])
        self.cnt[e] += 1
        ins.then_inc(self.sem[e], 1)
        self._commit(("e", e, self.cnt[e]), r, w)
        self.n_inst += 1

    def dma(self, q, out, in_, r=(), w=(), **kw):
        sem = None
        for b in list(w) + list(r):
            if b.dsem is not None:
                sem = b.dsem
                break
        assert sem is not None
        self._deps(q, r, w)
        ins = self.eng[q].dma_start(out=out, in_=in_, **kw)
        self.dcnt[sem] += 16
        ins.then_inc(self.dsem[sem], 16)
        self._commit(("d", sem, self.dcnt[sem]), r, w)
        self.n_inst += 1

    def barrier(self):
        for e in self.eng:
            for k in self.sem:
                if self.cnt[k] > 0 and k != e:
                    self._wait(e, ("e", k, self.cnt[k]))
            for k in range(len(self.dsem)):
                if self.dcnt[k] > 0:
                    self._wait(e, ("d", k, self.dcnt[k]))

    def mm(self, out, lhsT, rhs, start, stop, r, w):
        self.op("pe", lambda e: e.matmul(out, lhsT=lhsT, rhs=rhs, start=start, stop=stop), r, w)

    def tr(self, out, in_, ident, r, w):
        self.op("pe", lambda e: e.transpose(out=out, in_=in_, identity=ident), r, w)

    def act(self, out, in_, func, r, w, **kw):
        self.op("act", lambda e: e.activation(out=out, in_=in_, func=func, **kw), r, w)

    def tt(self, eng, out, in0, in1, op, r, w):
        self.op(eng, lambda e: e.tensor_tensor(out=out, in0=in0, in1=in1, op=op), r, w)

    def ts(self, eng, out, in0, s1, s2, op0, op1, r, w):
        if op1 is None:
            self.op(eng, lambda e: e.tensor_scalar(out=out, in0=in0, scalar1=s1, scalar2=None, op0=op0), r, w)
        else:
            self.op(eng, lambda e: e.tensor_scalar(out=out, in0=in0, scalar1=s1, scalar2=s2, op0=op0, op1=op1), r, w)

    def stt(self, out, in0, scalar, in1, op0, op1, r, w):
        self.op("dve", lambda e: e.scalar_tensor_tensor(out=out, in0=in0, scalar=scalar, in1=in1, op0=op0, op1=op1), r, w)

    def copy(self, eng, out, in_, r, w):
        if eng == "act":
            self.act(out, in_, AF.Copy, r, w)
        else:
            self.op(eng, lambda e: e.tensor_copy(out=out, in_=in_), r, w)


def bc(ap, shape):
    return ap.to_broadcast(list(shape))


def build_program(S, depth=DEPTH, dbg=(), phases=('P1','P2a','P2b','P2c','P3','P3b','P4','P5','P6','P7')):
    NT = S // 128
    TG = min(512, S)
    NG = S // TG
    TPG = TG // 128
    nc = bass.Bass("TRN2", target_bir_lowering=False)

    def din(name, shape, dt=F32):
        return nc.dram_tensor(name, list(shape), dt, kind="ExternalInput").ap()

    x_in = din("x", [S, D])
    norm1_g = din("norm1_g", [depth, D])
    w_in = din("w_in", [depth, D, IN_COLS])
    conv_w = din("conv_w", [depth, 128, 24 * 5])
    a_log = din("a_log", [depth, 16])
    dt_bias = din("dt_bias", [depth, 16])
    ret_gn_g = din("ret_gn_g", [depth, D])
    dn_norm_g = din("dn_norm_g", [depth, 128])
    w_proj_ret = din("w_proj_ret", [depth, D, D])
    w_proj_dn = din("w_proj_dn", [depth, D, D])
    w_out = din("w_out", [depth, D, D])
    norm2_g = din("norm2_g", [depth, D])
    w_ff1 = din("w_ff1", [depth, D, DFF])
    w_ff2 = din("w_ff2", [depth, DFF, D])
    final_g = din("final_g", [D])
    k_rot = din("k_rot", [2, 128, S])
    k_ret = din("k_ret", [5, 128, 4 * 128])
    k_dn = din("k_dn", [6, 128, 128])
    k_id = din("k_id", [128, 128])
    k_dnm = din("k_dnm", [14, 128, 128])
    out = nc.dram_tensor("out", [S, D], F32, kind="ExternalOutput").ap()

    with ExitStack() as es:
        c = Ctx(nc, es)

        def scr(name, shape, dt):
            kind = "ExternalOutput" if name in dbg else "Internal"
            return c.dram(name, shape, dt, kind=kind)

        xA = scr("xA", [S, D], F32)
        xM = scr("xM", [S, D], F32)
        rqT = scr("rqT", [4, 128, S], BF16)
        rkT = scr("rkT", [4, 128, S], BF16)
        rk_tm = scr("rk_tm", [S, 512], BF16)
        rv = scr("rv", [S, D], BF16)
        rg = scr("rg", [S, D], BF16)
        dqT = scr("dqT", [8, 128, S], BF16)
        dkT = scr("dkT", [8, 128, S], BF16)
        dk_tm = scr("dk_tm", [S, D], BF16)
        dv_tm = scr("dv_tm", [S, D], BF16)
        dz = scr("dz", [S, D], BF16)
        dbg_ = scr("dbg", [S, 32], F32)
        gatesT = scr("gatesT", [16, 128, S], BF16)
        yret = scr("yret", [S, D], BF16)
        o_f = scr("o_f", [S, D], F32)
        o_b = scr("o_b", [S, D], F32)

        def load_w(dst, b_dst, src, q="pool"):
            c.dma(q, dst, src, w=[b_dst])

        def rms_rstd(ss, b_ss, n):
            c.act(ss[:, 1:2], ss[:, 0:1], AF.Ln, [b_ss], [b_ss], scale=1.0 / n, bias=EPS)
            c.act(ss[:, 1:2], ss[:, 1:2], AF.Exp, [b_ss], [b_ss], scale=-0.5)

        for l in range(depth):
          try:
            x_src = x_in if l == 0 else xA
            last = (l == depth - 1)

            with c.open():
                hT = c.sb("hT", [128, 8, S], BF16)
                b_hT = [c.buf("hT%d" % t) for t in range(NT)]
                ident = c.sb("ident", [128, 128], BF16)
                identf = c.sb("identf", [128, 128], F32)
                b_id = c.buf("ident", dma=True)
                c.dma("sp", identf[:], k_id, w=[b_id])
                c.copy("dve", ident[:], identf[:], [b_id], [b_id])
                onesb = c.sb("onesb", [128, 128], BF16)
                b_ones = c.buf("ones")
                c.op("dve", lambda e: e.memset(onesb[:], 1.0), w=[b_ones])
                g1 = c.sb("g1", [128, D], F32)
                b_g1 = c.buf("g1", dma=True)
                c.dma("sp", g1[:], norm1_g[l].partition_broadcast(128), w=[b_g1])
                U = [c.sb("U%d" % i, [128, max(S, 4096) + 4], F32) for i in range(2)]
                b_U = [c.buf("U%d" % i) for i in range(2)]
                acc = [c.sb("acc%d" % i, [128, max(S, 4096)], F32) for i in range(2)]
                b_acc = [c.buf("acc%d" % i) for i in range(2)]
                for i in range(2):
                    c.op("dve", lambda e: e.memset(U[i][:], 0.0), w=[b_U[i]])
                xt = [acc[1][:, i * 1024:(i + 1) * 1024] for i in range(2)]
                b_xt = [c.buf("xt%d" % i, dma=True) for i in range(2)]
                _hbv = acc[1][:, 2048:3072].bitcast(BF16)
                junk = acc[1][:, 3072:3584].bitcast(BF16)
                b_junk = c.buf("junk")
                ss = [c.sb("ss%d" % i, [128, 2], F32) for i in range(2)]
                b_ss = [c.buf("ss%d" % i) for i in range(2)]
                hb = [_hbv[:, i * 1024:(i + 1) * 1024] for i in range(2)]
                b_hb = [c.buf("hb%d" % i) for i in range(2)]
                pT = [c.ps("pT%d" % i, [128, 8, 128], BF16) for i in range(1)] * 2
                b_pT = [c.buf("pT%d" % i) for i in range(1)] * 2
                for t in range(NT):
                    s = t % 2
                    c.dma("sp", xt[s][:], x_src[t * 128:(t + 1) * 128, :], w=[b_xt[s]])
                    c.act(junk[:], xt[s][:], AF.Square, [b_xt[s]], [b_junk, b_ss[s]], accum_out=ss[s][:, 0:1])
                    rms_rstd(ss[s], b_ss[s], D)
                    c.stt(hb[s][:], xt[s][:], ss[s][:, 1:2], g1[:], ALU.mult, ALU.mult,
                          [b_xt[s], b_ss[s], b_g1], [b_hb[s]])
                    for kc in range(8):
                        c.tr(pT[s][:, kc, :], hb[s][:, kc * 128:(kc + 1) * 128], ident[:], [b_hb[s], b_id], [b_pT[s]])
                    c.copy("act" if t % 2 else "dve", hT[:, :, t * 128:(t + 1) * 128], pT[s][:], [b_pT[s]], [b_hT[t]])
                c.barrier()

                P2on = 'P2' in phases
                wl = w_in[l].rearrange("(kc p) n -> p kc n", p=128)
                wfm = [c.sb("wfm%d" % i, [128, 8, 256], BF16) for i in range(2)]
                b_wfm = [c.buf("wfm%d" % i, dma=True) for i in range(2)]
                cw = c.sb("cw", [128, 24 * 5], F32)
                b_cw = c.buf("cw", dma=True)
                c.dma("sp", cw[:], conv_w[l], w=[b_cw])
                sqb = c.sb("sqb", [128, S], BF16)
                b_sqb = c.buf("sqb")
                stg = [c.sb("stg%d" % i, [128, S], BF16) for i in range(2)]
                b_stg = [c.buf("stg%d" % i, dma=True) for i in range(2)]
                rot = [c.sb("rot%d" % i, [128, 2, TG], F32) for i in range(2)]
                b_rot = [c.buf("rot%d" % i, dma=True) for i in range(2)]
                t1 = [c.sb("t1_%d" % i, [128, 2, TG], F32) for i in range(2)]
                b_t1 = [c.buf("t1_%d" % i) for i in range(2)]
                rtmp = [c.sb("rtmp%d" % i, [128, TG], F32) for i in range(2)]
                b_rtmp = [c.buf("rtmp%d" % i) for i in range(2)]
                tmst = [c.sb("tmst%d" % i, [128, TPG, 128], BF16) for i in range(2)]
                b_tmst = [c.buf("tmst%d" % i, dma=True) for i in range(2)]
                pA = [c.ps("pA%d" % i, [128, 2, 512], F32) for i in range(2)]
                b_pA = [c.buf("pA%d" % i) for i in range(2)]
                pN = [c.ps("pN%d" % i, [128, 512], F32) for i in range(1)]
                b_pN = [c.buf("pN%d" % i) for i in range(1)]
                pTm = [c.ps("pTm%d" % i, [128, 8, 128], BF16) for i in range(2)]
                b_pTm = [c.buf("pTm%d" % i) for i in range(2)]
                cnt = {"job": 0, "pa": 0, "rot": 0, "tm": 0, "rt": 0}

                def fm_matmul(ws, b_w, ntile, g):
                    p = cnt["pa"] % 2
                    cnt["pa"] += 1
                    for j in range(ntile):
                        for kc in range(8):
                            c.mm(pA[p][:, j, 0:TG], ws[:, kc, j * 128:(j + 1) * 128], hT[:, kc, g * TG:(g + 1) * TG],
                                 kc == 0, kc == 7, [b_w] + b_hT[g * TPG:(g + 1) * TPG], [b_pA[p]])
                    return p

                def fm_to_tm(st, b_st, dst, col0):
                    for g in range(NG):
                        p = cnt["tm"] % 2
                        cnt["tm"] += 1
                        for j in range(TPG):
                            t = g * TPG + j
                            c.tr(pTm[p][:, j, :], st[:, t * 128:(t + 1) * 128], ident[:], [b_st, b_id], [b_pTm[p]])
                        c.copy("act", tmst[p][:], pTm[p][:, 0:TPG, :], [b_pTm[p]], [b_tmst[p]])
                        c.dma("sp", dst[g * TG:(g + 1) * TG, col0:col0 + 128].rearrange("(j p) d -> p j d", p=128),
                              tmst[p][:], r=[b_tmst[p]])

                def next_w(ntile, cols):
                    s = cnt["job"] % 2
                    cnt["job"] += 1
                    off = 0
                    for (c0, n) in cols:
                        load_w(wfm[s][:, :, off:off + n], b_wfm[s], wl[:, :, c0:c0 + n])
                        off += n
                    return s

                for (cbase, dstT, tmdst) in ((C_RQ, rqT, None), (C_RK, rkT, rk_tm)):
                    if 'P2a' not in phases:
                        break
                    for h in range(4):
                        c0 = cbase + h * 128
                        s = next_w(2, [(c0, 128), (c0 + 64, 64), (c0, 64)])
                        so = cnt["rt"] % 2
                        cnt["rt"] += 1
                        for g in range(NG):
                            rs = cnt["rot"] % 2
                            cnt["rot"] += 1
                            c.dma("sp", rot[rs][:], k_rot[:, :, g * TG:(g + 1) * TG].rearrange("a p s -> p a s"),
                                  w=[b_rot[rs]])
                            p = fm_matmul(wfm[s], b_wfm[s], 2, g)
                            c.tt("dve", t1[rs][:], pA[p][:, :, 0:TG], rot[rs][:], ALU.mult,
                                 [b_pA[p], b_rot[rs]], [b_t1[rs]])
                            c.tt("pool", stg[so][:, g * TG:(g + 1) * TG], t1[rs][:, 0, :], t1[rs][:, 1, :], ALU.add,
                                 [b_t1[rs]], [b_stg[so]])
                        c.dma("sp", dstT[h], stg[so][:], r=[b_stg[so]])
                        if tmdst is not None:
                            fm_to_tm(stg[so], b_stg[so], tmdst, h * 128)

                conv_jobs = []
                if 'P2b' in phases:
                    for (kind, cbase, dstT, tmdst) in (("q", C_DQ, dqT, None), ("k", C_DK, dkT, dk_tm), ("v", C_DV, None, dv_tm)):
                        for h in range(8):
                            conv_jobs.append((kind, cbase, dstT, tmdst, h))

                def conv_s1(j):
                    kind, cbase, dstT, tmdst, h = conv_jobs[j]
                    u = j % 2
                    s = next_w(1, [(cbase + h * 128, 128)])
                    for g in range(NG):
                        p = fm_matmul(wfm[s], b_wfm[s], 1, g)
                        c.copy("act", U[u][:, 2 + g * TG:2 + (g + 1) * TG], pA[p][:, 0, 0:TG], [b_pA[p]], [b_U[u]])

                def conv_s2(j):
                    kind, cbase, dstT, tmdst, h = conv_jobs[j]
                    u = j % 2
                    Uu, au, b_Uu, b_au = U[u], acc[u][:, 0:S], b_U[u], b_acc[u]
                    ti = {"q": 0, "k": 8, "v": 16}[kind] + h
                    so = cnt["rt"] % 2
                    cnt["rt"] += 1
                    c.ts("dve", au, Uu[:, 0:S], cw[:, ti * 5:ti * 5 + 1], None, ALU.mult, None, [b_Uu, b_cw], [b_au])
                    for k in range(1, 5):
                        c.stt(au, Uu[:, k:k + S], cw[:, ti * 5 + k:ti * 5 + k + 1], au, ALU.mult, ALU.add,
                              [b_Uu, b_cw, b_au], [b_au])
                    if kind == "v":
                        c.act(stg[so][:], au, AF.Silu, [b_au], [b_stg[so]])
                    else:
                        c.act(au, au, AF.Silu, [b_au], [b_au])
                        c.act(sqb[:], au, AF.Square, [b_au], [b_sqb])
                        for g in range(NG):
                            r2 = cnt["rot"] % 2
                            cnt["rot"] += 1
                            c.mm(pN[0][:, 0:TG], onesb[:], sqb[:, g * TG:(g + 1) * TG], True, True,
                                 [b_ones, b_sqb], [b_pN[0]])
                            c.act(rtmp[r2][:], pN[0][:, 0:TG], AF.Ln, [b_pN[0]], [b_rtmp[r2]], bias=EPS)
                            c.act(rtmp[r2][:], rtmp[r2][:], AF.Exp, [b_rtmp[r2]], [b_rtmp[r2]], scale=-0.5)
                            c.stt(stg[so][:, g * TG:(g + 1) * TG], au[:, g * TG:(g + 1) * TG],
                                  (128.0 ** -0.5) if kind == "q" else 1.0, rtmp[r2][:], ALU.mult, ALU.mult,
                                  [b_au, b_rtmp[r2]], [b_stg[so]])
                    if dstT is not None:
                        c.dma("sp", dstT[h], stg[so][:], r=[b_stg[so]])
                    if tmdst is not None:
                        fm_to_tm(stg[so], b_stg[so], tmdst, h * 128)

                if conv_jobs:
                    conv_s1(0)
                    for j in range(len(conv_jobs)):
                        if j + 1 < len(conv_jobs):
                            conv_s1(j + 1)
                        conv_s2(j)

                for f2 in range(8):
                    if 'P2c' not in phases:
                        break
                    s = next_w(2, [(C_GATE + f2 * 256, 256)])
                    sos = []
                    for j in range(2):
                        sos.append(cnt["rt"] % 2)
                        cnt["rt"] += 1
                    for g in range(NG):
                        p = fm_matmul(wfm[s], b_wfm[s], 2, g)
                        for j in range(2):
                            c.act(stg[sos[j]][:, g * TG:(g + 1) * TG], pA[p][:, j, 0:TG], AF.Sigmoid,
                                  [b_pA[p]], [b_stg[sos[j]]])
                    for j in range(2):
                        c.dma("sp", gatesT[f2 * 2 + j], stg[sos[j]][:], r=[b_stg[sos[j]]])

                c.barrier()
                _wv = U[1][:, 0:4096].bitcast(BF16)
                wtm = [_wv[:, i * 4096:(i + 1) * 4096].rearrange("p (k n) -> p k n", k=8) for i in range(2)]
                b_wtm = [c.buf("wtm%d" % i, dma=True) for i in range(2)]
                ost = [c.sb("ost%d" % i, [128, 512], BF16) for i in range(3)]
                b_ost = [c.buf("ost%d" % i, dma=True) for i in range(3)]
                alt = c.sb("alt", [128, 2, 16], F32)
                b_alt = c.buf("alt", dma=True)
                c.dma("sp", alt[:, 0, :], dt_bias[l].partition_broadcast(128), w=[b_alt])
                c.dma("sp", alt[:, 1, :], a_log[l].partition_broadcast(128), w=[b_alt])
                c.act(alt[:, 1, :], alt[:, 1, :], AF.Exp, [b_alt], [b_alt])
                c.ts("dve", alt[:, 1, :], alt[:, 1, :], -1.0, None, ALU.mult, None, [b_alt], [b_alt])
                bgs = [c.sb("bgs%d" % i, [128, 32], F32) for i in range(2)]
                b_bgs = [c.buf("bgs%d" % i, dma=True) for i in range(2)]
                k3 = 0
                banks = [(C_RV, rv, 0, AF.Copy), (C_RV + 512, rv, 512, AF.Copy),
                         (C_RG, rg, 0, AF.Silu), (C_RG + 512, rg, 512, AF.Silu),
                         (C_DZ, dz, 0, AF.Silu), (C_DZ + 512, dz, 512, AF.Silu)]
                for bi, (c0, dst, dc0, fn) in enumerate(banks):
                    if 'P3' not in phases:
                        break
                    s = bi % 2
                    load_w(wtm[s][:], b_wtm[s], wl[:, :, c0:c0 + 512])
                    for t in range(NT):
                        p = cnt["pa"] % 2
                        cnt["pa"] += 1
                        for kc in range(8):
                            c.mm(pA[p][:, 0, :], hT[:, kc, t * 128:(t + 1) * 128], wtm[s][:, kc, :], kc == 0, kc == 7,
                                 [b_hT[t], b_wtm[s]], [b_pA[p]])
                        o3 = k3 % 3
                        k3 += 1
                        c.act(ost[o3][:], pA[p][:, 0, :], fn, [b_pA[p]], [b_ost[o3]])
                        c.dma("sp", dst[t * 128:(t + 1) * 128, dc0:dc0 + 512], ost[o3][:], r=[b_ost[o3]])
                s = len(banks) % 2
                load_w(wtm[s][:, :, 0:32], b_wtm[s], wl[:, :, C_DB:C_DB + 32])
                for t in range(NT):
                    if 'P3b' not in phases:
                        break
                    p = cnt["pa"] % 2
                    cnt["pa"] += 1
                    for kc in range(8):
                        c.mm(pA[p][:, 0, 0:32], hT[:, kc, t * 128:(t + 1) * 128], wtm[s][:, kc, 0:32], kc == 0, kc == 7,
                             [b_hT[t], b_wtm[s]], [b_pA[p]])
                    q = t % 2
                    c.act(bgs[q][:, 0:16], pA[p][:, 0, 0:16], AF.Sigmoid, [b_pA[p]], [b_bgs[q]])
                    c.tt("dve", bgs[q][:, 16:32], pA[p][:, 0, 16:32], alt[:, 0, :], ALU.add, [b_pA[p], b_alt], [b_bgs[q]])
                    c.act(bgs[q][:, 16:32], bgs[q][:, 16:32], AF.Exp, [b_bgs[q]], [b_bgs[q]])
                    c.act(bgs[q][:, 16:32], bgs[q][:, 16:32], AF.Ln, [b_bgs[q]], [b_bgs[q]], bias=1.0)
                    c.tt("dve", bgs[q][:, 16:32], bgs[q][:, 16:32], alt[:, 1, :], ALU.mult, [b_bgs[q], b_alt], [b_bgs[q]])
                    c.dma("sp", dbg_[t * 128:(t + 1) * 128, :], bgs[q][:], r=[b_bgs[q]])
                c.barrier()

            with c.open():
                if 'P4' not in phases:
                    raise _Stop()
                gam = [1.0 - 2.0 ** (-5.0 - h) for h in range(4)]
                cdr = [float(np.float32(g_) ** 128) for g_ in gam]
                tab = c.sb("rtab", [128, 5, 512], F32)
                b_tab = c.buf("rtab", dma=True)
                c.dma("sp", tab[:], k_ret.rearrange("a p n -> p a n"), w=[b_tab])
                gng = c.sb("gng", [128, D], F32)
                b_gng = c.buf("gng", dma=True)
                c.dma("sp", gng[:], ret_gn_g[l].partition_broadcast(128), w=[b_gng])
                hist = c.sb("hist", [128, NT, 1024], BF16)
                b_hist = [c.buf("hist%d" % n) for n in range(NT)]
                Sb32 = c.sb("Sb32", [128, 1024], F32)
                Sf32 = c.sb("Sf32", [128, 1024], F32)
                b_Sb = c.buf("Sb32")
                b_Sf = c.buf("Sf32")
                c.op("dve", lambda e: e.memset(Sb32[:], 0.0), w=[b_Sb])
                c.op("dve", lambda e: e.memset(Sf32[:], 0.0), w=[b_Sf])
                Sfb = c.sb("Sfb", [128, 1024], BF16)
                b_Sfb = c.buf("Sfb")
                ktm = [c.sb("ktm%d" % i, [128, 512], BF16) for i in range(2)]
                b_ktm = [c.buf("ktm%d" % i, dma=True) for i in range(2)]
                vt = [c.sb("vt%d" % i, [128, D], BF16) for i in range(2)]
                b_vt = [c.buf("vt%d" % i, dma=True) for i in range(2)]
                kx = [c.sb("kx%d" % i, [128, 512], BF16) for i in range(2)]
                b_kx = [c.buf("kx%d" % i) for i in range(2)]
                pKV = [c.ps("pKV%d" % i, [128, 1024], F32) for i in range(1)]
                b_pKV = [c.buf("pKV%d" % i) for i in range(1)]
                for i, n in enumerate(range(NT - 1, -1, -1)):
                    s = i % 2
                    c.dma("sp", ktm[s][:], rk_tm[n * 128:(n + 1) * 128, :], w=[b_ktm[s]])
                    c.dma("sp", vt[s][:], rv[n * 128:(n + 1) * 128, :], w=[b_vt[s]])
                    c.copy("act", hist[:, n, :], Sb32[:], [b_Sb], [b_hist[n]])
                    c.tt("pool", kx[s][:], ktm[s][:], tab[:, 2, :], ALU.mult, [b_ktm[s], b_tab], [b_kx[s]])
                    for h in range(4):
                        c.mm(pKV[0][:, h * 256:(h + 1) * 256], kx[s][:, h * 128:(h + 1) * 128], vt[s][:, h * 256:(h + 1) * 256],
                             True, True, [b_kx[s], b_vt[s]], [b_pKV[0]])
                    for h in range(4):
                        c.stt(Sb32[:, h * 256:(h + 1) * 256], Sb32[:, h * 256:(h + 1) * 256], cdr[h],
                              pKV[0][:, h * 256:(h + 1) * 256], ALU.mult, ALU.add, [b_Sb, b_pKV[0]], [b_Sb])
                qT = [c.sb("qT%d" % i, [128, 4, 128], BF16) for i in range(2)]
                b_qT = [c.buf("qT%d" % i, dma=True) for i in range(2)]
                kT = [c.sb("kT%d" % i, [128, 4, 128], BF16) for i in range(2)]
                b_kT = [c.buf("kT%d" % i, dma=True) for i in range(2)]
                rgt = [c.sb("rgt%d" % i, [128, D], BF16) for i in range(2)]
                b_rgt = [c.buf("rgt%d" % i, dma=True) for i in range(2)]
                sm = [c.sb("sm%d" % i, [128, 512], BF16) for i in range(2)]
                b_sm = [c.buf("sm%d" % i) for i in range(2)]
                qf = [c.sb("qf%d" % i, [128, 512], BF16) for i in range(2)]
                b_qf = [c.buf("qf%d" % i) for i in range(2)]
                qb = [c.sb("qb%d" % i, [128, 512], BF16) for i in range(2)]
                b_qb = [c.buf("qb%d" % i) for i in range(2)]
                st6 = [c.sb("st6_%d" % i, [128, 4, 6], F32) for i in range(2)]
                b_st6 = [c.buf("st6_%d" % i) for i in range(2)]
                mv = [c.sb("mv%d" % i, [128, 4, 2], F32) for i in range(2)]
                b_mv = [c.buf("mv%d" % i) for i in range(2)]
                rs4 = [c.sb("rs4_%d" % i, [128, 4], F32) for i in range(2)]
                b_rs4 = [c.buf("rs4_%d" % i) for i in range(2)]
                yn = [c.sb("yn%d" % i, [128, D], F32) for i in range(2)]
                b_yn = [c.buf("yn%d" % i) for i in range(2)]
                gs = [c.sb("gs%d" % i, [128, D], F32) for i in range(2)]
                b_gs = [c.buf("gs%d" % i) for i in range(2)]
                yo = [c.sb("yo%d" % i, [128, D], BF16) for i in range(2)]
                b_yo = [c.buf("yo%d" % i, dma=True) for i in range(2)]
                pSc = [c.ps("pSc%d" % i, [128, 512], F32) for i in range(1)]
                b_pSc = [c.buf("pSc%d" % i) for i in range(1)]
                pO = [c.ps("pO%d" % i, [128, 1024], F32) for i in range(2)]
                b_pO = [c.buf("pO%d" % i) for i in range(2)]
                rq_v = rqT.rearrange("h p s -> p h s")
                rk_v = rkT.rearrange("h p s -> p h s")
                for n in range(NT):
                    s = n % 2
                    tk = slice(n * 128, (n + 1) * 128)
                    c.dma("sp", qT[s][:], rq_v[:, :, tk], w=[b_qT[s]])
                    c.dma("sp", kT[s][:], rk_v[:, :, tk], w=[b_kT[s]])
                    c.dma("sp", ktm[s][:], rk_tm[tk, :], w=[b_ktm[s]])
                    c.dma("sp", vt[s][:], rv[tk, :], w=[b_vt[s]])
                    c.dma("sp", rgt[s][:], rg[tk, :], w=[b_rgt[s]])
                    c.copy("act", Sfb[:], Sf32[:], [b_Sf], [b_Sfb])
                    for h in range(4):
                        c.mm(pSc[0][:, h * 128:(h + 1) * 128], kT[s][:, h, :], qT[s][:, h, :], True, True,
                             [b_kT[s], b_qT[s]], [b_pSc[0]])
                    c.tt("dve", sm[s][:], pSc[0][:], tab[:, 0, :], ALU.mult, [b_pSc[0], b_tab], [b_sm[s]])
                    qflat = qT[s][:].rearrange("p h d -> p (h d)")
                    c.tt("pool", qf[s][:], qflat, tab[:, 3, :], ALU.mult, [b_qT[s], b_tab], [b_qf[s]])
                    c.tt("pool", qb[s][:], qflat, tab[:, 4, :], ALU.mult, [b_qT[s], b_tab], [b_qb[s]])
                    c.tt("pool", kx[s][:], ktm[s][:], tab[:, 1, :], ALU.mult, [b_ktm[s], b_tab], [b_kx[s]])
                    for h in range(4):
                        vs = slice(h * 256, (h + 1) * 256)
                        hs = slice(h * 128, (h + 1) * 128)
                        c.mm(pO[s][:, vs], sm[s][:, hs], vt[s][:, vs], True, False, [b_sm[s], b_vt[s]], [b_pO[s]])
                        c.mm(pO[s][:, vs], qf[s][:, hs], Sfb[:, vs], False, False, [b_qf[s], b_Sfb], [b_pO[s]])
                        c.mm(pO[s][:, vs], qb[s][:, hs], hist[:, n, vs], False, True, [b_qb[s], b_hist[n]], [b_pO[s]])
                    for h in range(4):
                        c.mm(pKV[0][:, h * 256:(h + 1) * 256], kx[s][:, h * 128:(h + 1) * 128], vt[s][:, h * 256:(h + 1) * 256],
                             True, True, [b_kx[s], b_vt[s]], [b_pKV[0]])
                    for h in range(4):
                        c.stt(Sf32[:, h * 256:(h + 1) * 256], Sf32[:, h * 256:(h + 1) * 256], cdr[h],
                              pKV[0][:, h * 256:(h + 1) * 256], ALU.mult, ALU.add, [b_Sf, b_pKV[0]], [b_Sf])
                    for h in range(4):
                        c.op("dve", lambda e: e.bn_stats(out=st6[s][:, h, :], in_=pO[s][:, h * 256:(h + 1) * 256]),
                             [b_pO[s]], [b_st6[s]])
                    for h in range(4):
                        c.op("dve", lambda e: e.bn_aggr(out=mv[s][:, h, :], in_=st6[s][:, h, :]), [b_st6[s]], [b_mv[s]])
                    c.act(rs4[s][:], mv[s][:, :, 1], AF.Ln, [b_mv[s]], [b_rs4[s]], bias=EPS)
                    c.act(rs4[s][:], rs4[s][:], AF.Exp, [b_rs4[s]], [b_rs4[s]], scale=-0.5)
                    for h in range(4):
                        c.ts("dve", yn[s][:, h * 256:(h + 1) * 256], pO[s][:, h * 256:(h + 1) * 256],
                             mv[s][:, h, 0:1], rs4[s][:, h:h + 1], ALU.subtract, ALU.mult,
                             [b_pO[s], b_mv[s], b_rs4[s]], [b_yn[s]])
                    c.tt("pool", gs[s][:], rgt[s][:], gng[:], ALU.mult, [b_rgt[s], b_gng], [b_gs[s]])
                    c.tt("pool", yo[s][:], yn[s][:], gs[s][:], ALU.mult, [b_yn[s], b_gs[s]], [b_yo[s]])
                    c.dma("sp", yret[tk, :], yo[s][:], r=[b_yo[s]])
                c.barrier()

            with c.open():
                if 'P5' not in phases:
                    raise _Stop()
                ident = c.sb("ident", [128, 128], BF16)
                identf = c.sb("identf", [128, 128], F32)
                b_id = c.buf("ident", dma=True)
                c.dma("sp", identf[:], k_id, w=[b_id])
                c.copy("dve", ident[:], identf[:], [b_id], [b_id])
                onesf = c.sb("onesf", [128, 128], F32)
                b_ones = c.buf("onesf")
                c.op("dve", lambda e: e.memset(onesf[:], 1.0), w=[b_ones])
                dtab = c.sb("dtab", [128, 6, 128], F32)
                b_dtab = c.buf("dtab", dma=True)
                c.dma("sp", dtab[:], k_dn.rearrange("a p n -> p a n"), w=[b_dtab])
                mtab = c.sb("mtab", [128, 14, 128], F32)
                b_mtab = c.buf("mtab", dma=True)
                c.dma("sp", mtab[:], k_dnm.rearrange("a p n -> p a n"), w=[b_mtab])
                S32 = [c.sb("S32_%d" % i, [128, 8, 128], F32) for i in range(2)]
                Sbf = [c.sb("Sbf_%d" % i, [128, 8, 128], BF16) for i in range(2)]
                b_S32 = [[c.buf("S32_%d_%d" % (i, hf)) for hf in range(2)] for i in range(2)]
                b_Sbf = [[c.buf("Sbf_%d_%d" % (i, hf)) for hf in range(2)] for i in range(2)]
                for i in range(2):
                    c.op("dve", lambda e: e.memset(S32[i][:], 0.0), w=b_S32[i])
                    c.op("dve", lambda e: e.memset(Sbf[i][:], 0.0), w=b_Sbf[i])

                def two(name, shape, dt, dma=False):
                    return ([c.sb("%s%d" % (name, i), shape, dt) for i in range(2)],
                            [c.buf("%s%d" % (name, i), dma=dma) for i in range(2)])

                def twoh(name, shape, dt):
                    return ([c.sb("%s%d" % (name, i), shape, dt) for i in range(2)],
                            [[c.buf("%s%d_%d" % (name, i, hf)) for hf in range(2)] for i in range(2)])
                qT, b_qT = two("dqT", [128, 8, 128], BF16, True)
                kT, b_kT = two("dkT", [128, 8, 128], BF16, True)
                ktm, b_ktm = two("dktm", [128, 8, 128], BF16, True)
                vtm, b_vtm = two("dvtm", [128, 8, 128], BF16, True)
                bg, b_bg = two("bg", [128, 32], F32, True)
                rhsG, b_rhsG = twoh("rhsG", [128, 8, 128], F32)
                sc, b_sc = twoh("sc", [128, 8, 8], F32)
                X, b_X = twoh("X", [128, 8, 128], F32)
                E, b_E = twoh("E", [128, 8, 128], F32)
                E3, b_E3 = twoh("E3", [128, 8, 128], F32)
                EG, b_EG = twoh("EG", [128, 8, 128], F32)
                P0, b_P0 = twoh("P0", [128, 8, 128], BF16)
                attn, b_attn = twoh("attn", [128, 8, 128], BF16)
                QA, b_QA = twoh("QA", [128, 16, 128], BF16)
                Tb, b_Tb = twoh("Tb", [128, 8, 128], BF16)
                Ttb, b_Ttb = twoh("Ttb", [128, 8, 128], BF16)
                X1m, b_X1m = twoh("X1m", [128, 8, 128], BF16)
                bv, b_bv = twoh("bv", [128, 8, 128], BF16)
                bk, b_bk = twoh("bk", [128, 8, 128], BF16)
                usb, b_usb = twoh("usb", [128, 8, 128], F32)
                wTn, b_wTn = twoh("wTn", [128, 8, 128], BF16)
                qdT, b_qdT = twoh("qdT", [128, 8, 128], BF16)
                kd, b_kd = twoh("kd", [128, 8, 128], BF16)
                dl, b_dl = twoh("dl", [128, 8, 128], BF16)
                osb, b_osb = two("osb", [128, 1024], F32, True)
                pG = c.ps("pG", [128, 8, 128], F32)
                pK = c.ps("pK", [128, 8, 128], F32)
                pQ = c.ps("pQ", [128, 8, 128], F32)
                pX = c.ps("pX", [128, 16, 128], BF16)
                b_pG = [c.buf("pG%d" % i) for i in range(2)]
                b_pK = [c.buf("pK%d" % i) for i in range(2)]
                b_pQ = [c.buf("pQ%d" % i) for i in range(2)]
                b_pX = [c.buf("pX%d" % i) for i in range(2)]
                dq_v = dqT.rearrange("h p s -> p h s")
                dk_v = dkT.rearrange("h p s -> p h s")
                HF = (0, 1)
                it = 0
                for t in range(NT):
                    for dr in range(2):
                        n = t if dr == 0 else NT - 1 - t
                        s = it % 2
                        it += 1
                        tk = slice(n * 128, (n + 1) * 128)
                        last_i = 127 if dr == 0 else 0
                        tri = dtab[:, 0 + dr, :]
                        lim = dtab[:, 2 + dr, :]
                        strict = dtab[:, 4 + dr, :]
                        c.dma("sp", qT[s][:], dq_v[:, :, tk], w=[b_qT[s]])
                        c.dma("sp", kT[s][:], dk_v[:, :, tk], w=[b_kT[s]])
                        c.dma("sp", ktm[s][:], dk_tm[tk, :].rearrange("p (h d) -> p h d", h=8), w=[b_ktm[s]])
                        c.dma("sp", vtm[s][:], dv_tm[tk, :].rearrange("p (h d) -> p h d", h=8), w=[b_vtm[s]])
                        c.dma("sp", bg[s][:], dbg_[tk, :], w=[b_bg[s]])
                        H = [slice(hf * 4, (hf + 1) * 4) for hf in HF]
                        beta = [bg[s][:, dr * 8 + hf * 4:dr * 8 + hf * 4 + 4] for hf in HF]
                        gl = [bg[s][:, 16 + dr * 8 + hf * 4:16 + dr * 8 + hf * 4 + 4] for hf in HF]
                        SH = [128, 4, 128]
                        for hf in HF:
                            c.tt("dve", rhsG[s][:, H[hf], :], bc(tri.unsqueeze(1), SH), bc(gl[hf].unsqueeze(2), SH),
                                 ALU.mult, [b_dtab, b_bg[s]], [b_rhsG[s][hf]])
                        for hf in HF:
                            c.mm(pG[:, H[hf], :], onesf[:], rhsG[s][:, H[hf], :], True, True,
                                 [b_ones, b_rhsG[s][hf]], [b_pG[hf]])
                            c.mm(pQ[:, hf * 4, 0:4], tri, gl[hf], True, True, [b_dtab, b_bg[s]], [b_pQ[hf]])
                        for hf in HF:
                            c.copy("act", sc[s][:, H[hf], 0], pQ[:, hf * 4, 0:4], [b_pQ[hf]], [b_sc[s][hf]])
                        for hf in HF:
                            gcs = sc[s][:, H[hf], 0]
                            c.tt("dve", X[s][:, H[hf], :], bc(gcs.unsqueeze(2), SH), pG[:, H[hf], :], ALU.subtract,
                                 [b_sc[s][hf], b_pG[hf]], [b_X[s][hf]])
                            c.tt("pool", X[s][:, H[hf], :], X[s][:, H[hf], :], bc(lim.unsqueeze(1), SH), ALU.add,
                                 [b_X[s][hf], b_dtab], [b_X[s][hf]])
                        for hf in HF:
                            gcs = sc[s][:, H[hf], 0]
                            c.act(E[s][:, H[hf], :], X[s][:, H[hf], :], AF.Exp, [b_X[s][hf]], [b_E[s][hf]])
                            c.act(EG[s][:, H[hf], :], pG[:, H[hf], :], AF.Exp, [b_pG[hf]], [b_EG[s][hf]])
                            c.act(sc[s][:, H[hf], 1], gcs, AF.Exp, [b_sc[s][hf]], [b_sc[s][hf]])
                            c.act(sc[s][:, H[hf], 5], pG[:, H[hf], last_i], AF.Exp, [b_pG[hf]], [b_sc[s][hf]])
                            c.tt("dve", sc[s][:, H[hf], 4], pG[:, H[hf], last_i], gcs, ALU.subtract,
                                 [b_pG[hf], b_sc[s][hf]], [b_sc[s][hf]])
                            c.act(sc[s][:, H[hf], 4], sc[s][:, H[hf], 4], AF.Exp, [b_sc[s][hf]], [b_sc[s][hf]])
                            c.tt("pool", sc[s][:, H[hf], 2], sc[s][:, H[hf], 1], beta[hf], ALU.mult,
                                 [b_sc[s][hf], b_bg[s]], [b_sc[s][hf]])
                            c.ts("pool", sc[s][:, H[hf], 3], beta[hf], -1.0, 0.0, ALU.mult, ALU.add, [b_bg[s]], [b_sc[s][hf]])
                        for hf in HF:
                            for hh in range(4):
                                h = hf * 4 + hh
                                c.mm(pK[:, h, :], kT[s][:, h, :], kT[s][:, h, :], True, True, [b_kT[s]], [b_pK[hf]])
                            for hh in range(4):
                                h = hf * 4 + hh
                                c.mm(pQ[:, h, :], qT[s][:, h, :], kT[s][:, h, :], True, True, [b_qT[s], b_kT[s]], [b_pQ[hf]])
                        for hf in HF:
                            c.tt("pool", E3[s][:, H[hf], :], E[s][:, H[hf], :], bc(strict.unsqueeze(1), SH), ALU.mult,
                                 [b_E[s][hf], b_dtab], [b_E3[s][hf]])
                            c.tt("pool", E3[s][:, H[hf], :], E3[s][:, H[hf], :], bc(sc[s][:, H[hf], 3].unsqueeze(2), SH), ALU.mult,
                                 [b_E3[s][hf], b_sc[s][hf]], [b_E3[s][hf]])
                        for hf in HF:
                            c.tt("dve", P0[s][:, H[hf], :], pK[:, H[hf], :], E3[s][:, H[hf], :], ALU.mult,
                                 [b_pK[hf], b_E3[s][hf]], [b_P0[s][hf]])
                            c.tt("dve", attn[s][:, H[hf], :], pQ[:, H[hf], :], E[s][:, H[hf], :], ALU.mult,
                                 [b_pQ[hf], b_E[s][hf]], [b_attn[s][hf]])
                        for hf in HF:
                            for hh in range(4):
                                c.tr(pX[:, hf * 8 + hh, :], P0[s][:, hf * 4 + hh, :], ident[:], [b_P0[s][hf], b_id], [b_pX[hf]])
                            for hh in range(4):
                                c.tr(pX[:, hf * 8 + 4 + hh, :], attn[s][:, hf * 4 + hh, :], ident[:], [b_attn[s][hf], b_id], [b_pX[hf]])
                        for hf in HF:
                            c.copy("act", QA[s][:, hf * 8:(hf + 1) * 8, :], pX[:, hf * 8:(hf + 1) * 8, :], [b_pX[hf]], [b_QA[s][hf]])
                        m_dir, m_opp = dr * 7, (1 - dr) * 7
                        for hf in HF:
                            Q0h = QA[s][:, hf * 8:hf * 8 + 4, :]
                            c.tt("pool", Tb[0][:, H[hf], :], P0[s][:, H[hf], :], bc(mtab[:, m_dir, :].unsqueeze(1), SH), ALU.mult,
                                 [b_P0[s][hf], b_mtab], [b_Tb[0][hf]])
                            c.tt("pool", Tb[0][:, H[hf], :], Tb[0][:, H[hf], :], bc(identf[:].unsqueeze(1), SH), ALU.add,
                                 [b_Tb[0][hf], b_id], [b_Tb[0][hf]])
                            c.tt("pool", Ttb[0][:, H[hf], :], Q0h, bc(mtab[:, m_opp, :].unsqueeze(1), SH), ALU.mult,
                                 [b_QA[s][hf], b_mtab], [b_Ttb[0][hf]])
                            c.tt("pool", Ttb[0][:, H[hf], :], Ttb[0][:, H[hf], :], bc(identf[:].unsqueeze(1), SH), ALU.add,
                                 [b_Ttb[0][hf], b_id], [b_Ttb[0][hf]])
                        for lev in range(1, 7):
                            cu, nx = (lev - 1) % 2, lev % 2
                            x1 = lev % 2
                            for hf in HF:
                                for hh in range(4):
                                    h = hf * 4 + hh
                                    c.mm(pG[:, h, :], QA[s][:, hf * 8 + hh, :], Tb[cu][:, h, :], True, True,
                                         [b_QA[s][hf], b_Tb[cu][hf]], [b_pG[hf]])
                            for hf in HF:
                                c.tt("dve", X1m[x1][:, H[hf], :], pG[:, H[hf], :], bc(mtab[:, m_dir + lev, :].unsqueeze(1), SH), ALU.mult,
                                     [b_pG[hf], b_mtab], [b_X1m[x1][hf]])
                            for hf in HF:
                                for hh in range(4):
                                    h = hf * 4 + hh
                                    if lev < 6:
                                        c.mm(pK[:, h, :], Ttb[cu][:, h, :], X1m[x1][:, h, :], True, False,
                                             [b_Ttb[cu][hf], b_X1m[x1][hf]], [b_pK[hf]])
                                        c.mm(pK[:, h, :], ident[:], Tb[cu][:, h, :], False, True,
                                             [b_id, b_Tb[cu][hf]], [b_pK[hf]])
                                    c.mm(pQ[:, h, :], X1m[x1][:, h, :], Ttb[cu][:, h, :], True, False,
                                         [b_Ttb[cu][hf], b_X1m[x1][hf]], [b_pQ[hf]])
                                    c.mm(pQ[:, h, :], ident[:], Ttb[cu][:, h, :], False, True,
                                         [b_id, b_Ttb[cu][hf]], [b_pQ[hf]])
                            for hf in HF:
                                if lev < 6:
                                    c.copy("act", Tb[nx][:, H[hf], :], pK[:, H[hf], :], [b_pK[hf]], [b_Tb[nx][hf]])
                                c.copy("act", Ttb[nx][:, H[hf], :], pQ[:, H[hf], :], [b_pQ[hf]], [b_Ttb[nx][hf]])
                        Rf, b_Rf = Ttb[0], b_Ttb[0]
                        for hf in HF:
                            c.tt("pool", bv[s][:, H[hf], :], vtm[s][:, H[hf], :], bc(beta[hf].unsqueeze(2), SH), ALU.mult,
                                 [b_vtm[s], b_bg[s]], [b_bv[s][hf]])
                            c.tt("pool", bk[s][:, H[hf], :], ktm[s][:, H[hf], :], bc(sc[s][:, H[hf], 2].unsqueeze(2), SH), ALU.mult,
                                 [b_ktm[s], b_sc[s][hf]], [b_bk[s][hf]])
                            c.tt("pool", qdT[s][:, H[hf], :], qT[s][:, H[hf], :], EG[s][:, H[hf], :], ALU.mult,
                                 [b_qT[s], b_EG[s][hf]], [b_qdT[s][hf]])
                            c.tt("pool", kd[s][:, H[hf], :], ktm[s][:, H[hf], :], bc(sc[s][:, H[hf], 4].unsqueeze(2), SH), ALU.mult,
                                 [b_ktm[s], b_sc[s][hf]], [b_kd[s][hf]])
                        for hf in HF:
                            for hh in range(4):
                                h = hf * 4 + hh
                                c.mm(pG[:, h, :], Rf[:, h, :], bv[s][:, h, :], True, True, [b_Rf[hf], b_bv[s][hf]], [b_pG[hf]])
                            for hh in range(4):
                                h = hf * 4 + hh
                                c.mm(pK[:, h, :], bk[s][:, h, :], Rf[:, h, :], True, True, [b_Rf[hf], b_bk[s][hf]], [b_pK[hf]])
                        for hf in HF:
                            c.copy("act", usb[s][:, H[hf], :], pG[:, H[hf], :], [b_pG[hf]], [b_usb[s][hf]])
                            c.ts("dve", wTn[s][:, H[hf], :], pK[:, H[hf], :], -1.0, None, ALU.mult, None, [b_pK[hf]], [b_wTn[s][hf]])
                        for hf in HF:
                            for hh in range(4):
                                h = hf * 4 + hh
                                c.mm(pQ[:, h, :], wTn[s][:, h, :], Sbf[dr][:, h, :], True, True,
                                     [b_wTn[s][hf], b_Sbf[dr][hf]], [b_pQ[hf]])
                        for hf in HF:
                            c.tt("dve", dl[s][:, H[hf], :], usb[s][:, H[hf], :], pQ[:, H[hf], :], ALU.add,
                                 [b_usb[s][hf], b_pQ[hf]], [b_dl[s][hf]])
                        for hf in HF:
                            for hh in range(4):
                                h = hf * 4 + hh
                                c.mm(pG[:, h, :], qdT[s][:, h, :], Sbf[dr][:, h, :], True, False,
                                     [b_qdT[s][hf], b_Sbf[dr][hf]], [b_pG[hf]])
                                c.mm(pG[:, h, :], QA[s][:, hf * 8 + 4 + hh, :], dl[s][:, h, :], False, True,
                                     [b_QA[s][hf], b_dl[s][hf]], [b_pG[hf]])
                            for hh in range(4):
                                h = hf * 4 + hh
                                c.mm(pK[:, h, :], kd[s][:, h, :], dl[s][:, h, :], True, True, [b_kd[s][hf], b_dl[s][hf]], [b_pK[hf]])
                        for hf in HF:
                            c.copy("act", osb[s][:, hf * 512:(hf + 1) * 512], pG[:, H[hf], :].rearrange("p h d -> p (h d)"),
                                   [b_pG[hf]], [b_osb[s]])
                            c.tt("pool", S32[dr][:, H[hf], :], S32[dr][:, H[hf], :], bc(sc[s][:, H[hf], 5].unsqueeze(2), SH), ALU.mult,
                                 [b_S32[dr][hf], b_sc[s][hf]], [b_S32[dr][hf]])
                        c.dma("sp", (o_f if dr == 0 else o_b)[tk, :], osb[s][:], r=[b_osb[s]])
                        for hf in HF:
                            c.tt("dve", S32[dr][:, H[hf], :], S32[dr][:, H[hf], :], pK[:, H[hf], :], ALU.add,
                                 [b_S32[dr][hf], b_pK[hf]], [b_S32[dr][hf]])
                            c.copy("act", Sbf[dr][:, H[hf], :], S32[dr][:, H[hf], :], [b_S32[dr][hf]], [b_Sbf[dr][hf]])
                c.barrier()

            with c.open():
                if 'P6' not in phases:
                    raise _Stop()
                ident = c.sb("ident", [128, 128], BF16)
                identf = c.sb("identf", [128, 128], F32)
                b_id = c.buf("ident", dma=True)
                c.dma("sp", identf[:], k_id, w=[b_id])
                c.copy("dve", ident[:], identf[:], [b_id], [b_id])
                Wp = [c.sb("Wp%d" % i, [128, 8, D], BF16) for i in range(3)]
                b_Wp = [c.buf("Wp%d" % i, dma=True) for i in range(3)]
                for i, wsrc in enumerate((w_proj_ret, w_proj_dn, w_out)):
                    wv = wsrc[l].rearrange("(kc p) n -> p kc n", p=128)
                    for hh in range(2):
                        load_w(Wp[i][:, :, hh * 512:(hh + 1) * 512], b_Wp[i], wv[:, :, hh * 512:(hh + 1) * 512])
                dng = c.sb("dng", [128, 128], F32)
                b_dng = c.buf("dng", dma=True)
                c.dma("sp", dng[:], dn_norm_g[l].partition_broadcast(128), w=[b_dng])

                def two(name, shape, dt, dma=False):
                    return ([c.sb("%s%d" % (name, i), shape, dt) for i in range(2)],
                            [c.buf("%s%d" % (name, i), dma=dma) for i in range(2)])
                yr, b_yr = two("yr", [128, D], BF16, True)
                of_, b_of = two("of", [128, 8, 128], F32, True)
                ob_, b_ob = two("ob", [128, 8, 128], F32, True)
                dzt, b_dzt = two("dzt", [128, 8, 128], BF16, True)
                xt, b_xt = two("xt6", [128, D], F32, True)
                gt, b_gt = two("gt", [128, 16, 128], BF16, True)
                sq, b_sq = two("sq6", [128, 8, 128], F32)
                s8, b_s8 = two("s8", [128, 2, 8], F32)
                gz, b_gz = two("gz", [128, 8, 128], F32)
                yd, b_yd = two("yd", [128, 8, 128], BF16)
                yT, b_yT = two("yT", [128, 16, 128], BF16)
                m1, b_m1 = two("m1", [128, 16, 128], F32)
                mT, b_mT = two("mT", [128, 8, 128], BF16)
                xo, b_xo = two("xo", [128, D], F32, True)
                pY = c.ps("pY", [128, 16, 128], BF16)
                pP = c.ps("pP", [128, 16, 128], F32)
                pXo = c.ps("pXo", [128, D], F32)
                b_pY, b_pP, b_pXo = c.buf("pY"), c.buf("pP"), c.buf("pXo")
                g_v = gatesT.rearrange("f p s -> p f s")

                def p6_prep(n):
                    s = n % 2
                    tk = slice(n * 128, (n + 1) * 128)
                    c.dma("sp", yr[s][:], yret[tk, :], w=[b_yr[s]])
                    c.dma("sp", of_[s][:], o_f[tk, :].rearrange("p (h d) -> p h d", h=8), w=[b_of[s]])
                    c.dma("sp", ob_[s][:], o_b[tk, :].rearrange("p (h d) -> p h d", h=8), w=[b_ob[s]])
                    c.dma("sp", dzt[s][:], dz[tk, :].rearrange("p (h d) -> p h d", h=8), w=[b_dzt[s]])
                    c.dma("sp", xt[s][:], x_src[tk, :], w=[b_xt[s]])
                    c.dma("sp", gt[s][:], g_v[:, :, tk], w=[b_gt[s]])
                    c.tt("pool", of_[s][:], of_[s][:], ob_[s][:], ALU.add, [b_of[s], b_ob[s]], [b_of[s]])
                    c.act(sq[s][:], of_[s][:], AF.Square, [b_of[s]], [b_sq[s]])
                    c.op("dve", lambda e: e.tensor_reduce(out=s8[s][:, 0, :], in_=sq[s][:], axis=AX.X, op=ALU.add),
                         [b_sq[s]], [b_s8[s]])
                    c.act(s8[s][:, 1, :], s8[s][:, 0, :], AF.Ln, [b_s8[s]], [b_s8[s]], scale=1.0 / 128, bias=EPS)
                    c.act(s8[s][:, 1, :], s8[s][:, 1, :], AF.Exp, [b_s8[s]], [b_s8[s]], scale=-0.5)
                    c.tt("pool", gz[s][:], dzt[s][:], bc(dng[:].unsqueeze(1), [128, 8, 128]), ALU.mult,
                         [b_dzt[s], b_dng], [b_gz[s]])
                    c.tt("dve", sq[s][:], of_[s][:], bc(s8[s][:, 1, :].unsqueeze(2), [128, 8, 128]), ALU.mult,
                         [b_of[s], b_s8[s]], [b_sq[s]])
                    c.tt("pool", yd[s][:], sq[s][:], gz[s][:], ALU.mult, [b_sq[s], b_gz[s]], [b_yd[s]])
                    for kc in range(8):
                        c.tr(pY[:, kc, :], yr[s][:, kc * 128:(kc + 1) * 128], ident[:], [b_yr[s], b_id], [b_pY])
                    for kc in range(8):
                        c.tr(pY[:, 8 + kc, :], yd[s][:, kc, :], ident[:], [b_yd[s], b_id], [b_pY])
                    c.copy("act", yT[s][:], pY[:], [b_pY], [b_yT[s]])

                def p6_main(n):
                    s = n % 2
                    tk = slice(n * 128, (n + 1) * 128)
                    for br in range(2):
                        for do in range(8):
                            for kc in range(8):
                                c.mm(pP[:, br * 8 + do, :], Wp[br][:, kc, do * 128:(do + 1) * 128], yT[s][:, br * 8 + kc, :],
                                     kc == 0, kc == 7, [b_Wp[br], b_yT[s]], [b_pP])
                    c.tt("dve", m1[s][:], pP[:], gt[s][:], ALU.mult, [b_pP, b_gt[s]], [b_m1[s]])
                    c.tt("pool", mT[s][:], m1[s][:, 0:8, :], m1[s][:, 8:16, :], ALU.add, [b_m1[s]], [b_mT[s]])
                    for hh in range(2):
                        for kc in range(8):
                            c.mm(pXo[:, hh * 512:(hh + 1) * 512], mT[s][:, kc, :], Wp[2][:, kc, hh * 512:(hh + 1) * 512],
                                 kc == 0, kc == 7, [b_mT[s], b_Wp[2]], [b_pXo])
                    c.tt("dve", xo[s][:], xt[s][:], pXo[:], ALU.add, [b_xt[s], b_pXo], [b_xo[s]])
                    c.dma("sp", xM[tk, :], xo[s][:], r=[b_xo[s]])

                p6_prep(0)
                for n in range(NT):
                    if n + 1 < NT:
                        p6_prep(n + 1)
                    p6_main(n)
                c.barrier()

            with c.open():
                if 'P7' not in phases:
                    raise _Stop()
                ident = c.sb("ident", [128, 128], BF16)
                identf = c.sb("identf", [128, 128], F32)
                b_id = c.buf("ident", dma=True)
                c.dma("sp", identf[:], k_id, w=[b_id])
                c.copy("dve", ident[:], identf[:], [b_id], [b_id])
                W1 = c.sb("W1", [128, 8, DFF], BF16)
                W2 = c.sb("W2", [128, 32, D], BF16)
                b_W1 = [c.buf("W1_%d" % i, dma=True) for i in range(4)]
                b_W2 = [c.buf("W2_%d" % i, dma=True) for i in range(4)]
                w1v = w_ff1[l].rearrange("(kc p) n -> p kc n", p=128)
                w2v = w_ff2[l].rearrange("(kc p) n -> p kc n", p=128)
                for i in range(4):
                    for j in range(2):
                        cs = slice(i * 1024 + j * 512, i * 1024 + (j + 1) * 512)
                        load_w(W1[:, :, cs], b_W1[i], w1v[:, :, cs])
                for i in range(4):
                    for j in range(2):
                        load_w(W2[:, i * 8:(i + 1) * 8, j * 512:(j + 1) * 512], b_W2[i],
                               w2v[:, i * 8:(i + 1) * 8, j * 512:(j + 1) * 512])
                g2 = c.sb("g2", [128, D], F32)
                b_g2 = c.buf("g2", dma=True)
                c.dma("sp", g2[:], norm2_g[l].partition_broadcast(128), w=[b_g2])
                if last:
                    gf = c.sb("gf", [128, D], F32)
                    b_gf = c.buf("gf", dma=True)
                    c.dma("sp", gf[:], final_g.partition_broadcast(128), w=[b_gf])

                def two(name, shape, dt, dma=False):
                    return ([c.sb("%s%d" % (name, i), shape, dt) for i in range(2)],
                            [c.buf("%s%d" % (name, i), dma=dma) for i in range(2)])
                xt, b_xt = two("xt7", [128, D], F32, True)
                junk = c.sb("junk7", [128, D], BF16)
                b_junk = c.buf("junk7")
                ss, b_ss = two("ss7", [128, 2], F32)
                hb, b_hb = two("hb7", [128, D], BF16)
                h2T, b_h2T = two("h2T", [128, 8, 128], BF16)
                rl, b_rl = two("rl", [128, 4, 128], F32)
                hid, b_hid = two("hid", [128, 32, 128], BF16)
                xo, b_xo = two("xo7", [128, D], F32, True)
                fo, b_fo = two("fo7", [128, D], F32, True)
                pT = c.ps("pT7", [128, 8, 128], BF16)
                pH = [c.ps("pH%d" % i, [128, 4, 128], F32) for i in range(4)]
                pO = c.ps("pO7", [128, D], F32)
                b_pT, b_pO = c.buf("pT7"), c.buf("pO7")
                b_pH = [c.buf("pH%d" % i) for i in range(4)]

                def p7_prep(n):
                    s = n % 2
                    tk = slice(n * 128, (n + 1) * 128)
                    c.dma("sp", xt[s][:], xM[tk, :], w=[b_xt[s]])
                    c.act(junk[:], xt[s][:], AF.Square, [b_xt[s]], [b_junk, b_ss[s]], accum_out=ss[s][:, 0:1])
                    rms_rstd(ss[s], b_ss[s], D)
                    c.stt(hb[s][:], xt[s][:], ss[s][:, 1:2], g2[:], ALU.mult, ALU.mult,
                          [b_xt[s], b_ss[s], b_g2], [b_hb[s]])
                    for kc in range(8):
                        c.tr(pT[:, kc, :], hb[s][:, kc * 128:(kc + 1) * 128], ident[:], [b_hb[s], b_id], [b_pT])
                    c.copy("dve", h2T[s][:], pT[:], [b_pT], [b_h2T[s]])

                def p7_main(n):
                    s = n % 2
                    tk = slice(n * 128, (n + 1) * 128)
                    for hq in range(8):
                        p = hq % 4
                        for j in range(4):
                            ht = hq * 4 + j
                            for kc in range(8):
                                c.mm(pH[p][:, j, :], W1[:, kc, ht * 128:(ht + 1) * 128], h2T[s][:, kc, :], kc == 0, kc == 7,
                                     [b_W1[ht // 8], b_h2T[s]], [b_pH[p]])
                        r2 = hq % 2
                        c.act(rl[r2][:], pH[p][:], AF.Relu, [b_pH[p]], [b_rl[r2]])
                        c.tt("pool" if hq % 2 else "dve", hid[s][:, hq * 4:(hq + 1) * 4, :], rl[r2][:], rl[r2][:], ALU.mult,
                             [b_rl[r2]], [b_hid[s]])
                    for hh in range(2):
                        for kc in range(32):
                            c.mm(pO[:, hh * 512:(hh + 1) * 512], hid[s][:, kc, :], W2[:, kc, hh * 512:(hh + 1) * 512],
                                 kc == 0, kc == 31, [b_hid[s], b_W2[kc // 8]], [b_pO])
                    c.tt("dve", xo[s][:], xt[s][:], pO[:], ALU.add, [b_xt[s], b_pO], [b_xo[s]])
                    if not last:
                        c.dma("sp", xA[tk, :], xo[s][:], r=[b_xo[s]])
                    else:
                        c.act(junk[:], xo[s][:], AF.Square, [b_xo[s]], [b_junk, b_ss[s]], accum_out=ss[s][:, 0:1])
                        rms_rstd(ss[s], b_ss[s], D)
                        c.stt(fo[s][:], xo[s][:], ss[s][:, 1:2], gf[:], ALU.mult, ALU.mult,
                              [b_xo[s], b_ss[s], b_gf], [b_fo[s]])
                        c.dma("sp", out[tk, :], fo[s][:], r=[b_fo[s]])

                p7_prep(0)
                for n in range(NT):
                    if n + 1 < NT:
                        p7_prep(n + 1)
                    p7_main(n)
                c.barrier()
          except _Stop:
            c.barrier()
            break
        n_inst = c.n_inst
    return nc, n_inst


def const_tables(S):
    f32 = np.float32
    half = 64
    inv = (f32(10000.0) ** (-np.arange(half, dtype=f32) / f32(half))).astype(f32)
    ang = (np.arange(S, dtype=f32)[:, None] * inv[None, :]).astype(f32)
    cos = np.cos(ang.astype(np.float64)).astype(f32).T
    sin = np.sin(ang.astype(np.float64)).astype(f32).T
    k_rot = np.stack([np.concatenate([cos, cos], 0), np.concatenate([-sin, sin], 0)], 0).astype(f32)
    idx = np.arange(128, dtype=np.float64)
    k_ret = np.zeros((5, 128, 4, 128), np.float64)
    sc = 128.0 ** -0.5
    for h in range(4):
        g = 1.0 - 2.0 ** (-5.0 - h)
        k_ret[0, :, h, :] = g ** np.abs(idx[:, None] - idx[None, :]) * sc
        k_ret[1, :, h, :] = (g ** (127.0 - idx))[:, None]
        k_ret[2, :, h, :] = (g ** idx)[:, None]
        k_ret[3, :, h, :] = (g ** (idx + 1.0))[None, :] * sc
        k_ret[4, :, h, :] = (g ** (128.0 - idx))[None, :] * sc
    k_ret = k_ret.reshape(5, 128, 512).astype(f32)
    a = np.arange(128)
    P, Fr = a[:, None], a[None, :]
    k_dn = np.stack([
        (P <= Fr), (P >= Fr),
        np.where(Fr <= P, 0.0, -BIG), np.where(Fr >= P, 0.0, -BIG),
        (Fr < P), (Fr > P)], 0).astype(f32)
    k_id = np.eye(128, dtype=f32)
    ms = []
    for lev in range(7):
        b = 1 << lev
        m = ((P // (2 * b) == Fr // (2 * b)) & ((P // b) % 2 == 1) & ((Fr // b) % 2 == 0))
        ms.append(m)
    k_dnm = np.stack(ms + [m.T for m in ms], 0).astype(f32)
    return {"k_rot": k_rot, "k_ret": k_ret, "k_dn": k_dn, "k_id": k_id, "k_dnm": k_dnm}


def make_in_maps(inputs, S, n_cores):
    depth = inputs["w_in"].shape[0]
    shared = {k: np.ascontiguousarray(v, dtype=np.float32) for k, v in inputs.items() if k != "x"}
    cwv = shared["conv_w"]
    shared["conv_w"] = np.ascontiguousarray(
        cwv.reshape(depth, 5, 24, 128).transpose(0, 3, 2, 1).reshape(depth, 128, 120))
    shared["a_log"] = shared["a_log"].reshape(depth, 16)
    shared["dt_bias"] = shared["dt_bias"].reshape(depth, 16)
    shared.update(const_tables(S))
    x = np.ascontiguousarray(inputs["x"], dtype=np.float32)
    maps = []
    for i in range(n_cores):
        m = dict(shared)
        m["x"] = x[i]
        maps.append(m)
    return maps


_PROG = {}


def kernel(**inputs):
    x = inputs["x"]
    B, S, _ = x.shape
    depth = inputs["w_in"].shape[0]
    key = (S, depth)
    if key not in _PROG:
        _PROG[key] = build_program(S, depth)[0]
    nc = _PROG[key]
    maps = make_in_maps(inputs, S, B)
    res = run_bass_kernel_spmd(nc, maps, core_ids=list(range(B)))
    return np.stack([np.asarray(r["out"], dtype=np.float32) for r in res.results], 0)
```

```python
import numpy as np
from contextlib import ExitStack
import concourse.bass as bass
import concourse.mybir as mybir
from concourse.bass_utils import run_bass_kernel_spmd

F32 = mybir.dt.float32
BF16 = mybir.dt.bfloat16
AF = mybir.ActivationFunctionType
ALU = mybir.AluOpType
AX = mybir.AxisListType

D = 1024
DEPTH = 2
NH_R, DK_R, DV_R = 4, 128, 256
NH_D = 8
DFF = 4096
IN_COLS = 9248
EPS = 1e-6
C_RQ, C_RK, C_RV, C_RG = 0, 512, 1024, 2048
C_DQ, C_DK, C_DV, C_DZ = 3072, 4096, 5120, 6144
C_DB, C_DA, C_GATE = 7168, 7184, 7200
BIG = 30000.0


class _Stop(Exception):
    pass


class Buf:
    __slots__ = ("name", "lw", "rd", "dsem", "psum")

    def __init__(self, name, dsem=None):
        self.name = name
        self.lw = None
        self.rd = []
        self.dsem = dsem
        self.psum = name.startswith("p") and name[1:2].isupper()


class Ctx:
    def __init__(self, nc, es, n_dsem=28):
        self.nc = nc
        self.es = es
        self.eng = {"pe": nc.tensor, "act": nc.scalar, "dve": nc.vector,
                    "pool": nc.gpsimd, "sp": nc.sync}
        self.sem, self.cnt = {}, {}
        for k in ("pe", "act", "dve", "pool"):
            self.sem[k] = es.enter_context(nc.semaphore("sem_" + k))
            self.cnt[k] = 0
        self.dsem, self.dcnt = [], []
        for i in range(n_dsem):
            self.dsem.append(es.enter_context(nc.semaphore("ds%d" % i)))
            self.dcnt.append(0)
        self.next_dsem = 0
        self.waited = {k: {} for k in self.eng}
        self.n_inst = 0
        self.scope = None
        self.uid = 0

    def dram(self, name, shape, dt, kind="Internal"):
        return self.nc.dram_tensor(name, list(shape), dt, kind=kind).ap()

    def open(self):
        self.scope = ExitStack()
        self.next_dsem = 0
        return self.scope

    def sb(self, name, shape, dt):
        self.uid += 1
        return self.scope.enter_context(self.nc.sbuf_tensor("%s_%d" % (name, self.uid), list(shape), dt))

    def ps(self, name, shape, dt):
        self.uid += 1
        return self.scope.enter_context(self.nc.psum_tensor("%s_%d" % (name, self.uid), list(shape), dt))

    def buf(self, name, dma=False):
        if dma:
            i = self.next_dsem
            self.next_dsem += 1
            assert i < len(self.dsem), "out of dma semaphores"
            return Buf(name, i)
        return Buf(name)

    def _wait(self, e, tok):
        kind, key, val = tok
        w = self.waited[e]
        if w.get((kind, key), 0) >= val:
            return
        w[(kind, key)] = val
        sem = self.sem[key] if kind == "e" else self.dsem[key]
        self.eng[e].wait_ge(sem, val)

    def _deps(self, e, r, w):
        toks = []
        raw = set()
        for b in r:
            if b.lw is not None:
                toks.append(b.lw)
                raw.add(b.lw)
        for b in w:
            if b.lw is not None:
                toks.append(b.lw)
            toks.extend(b.rd)
        for tok in toks:
            if tok[0] == "e" and tok[1] == e and tok not in raw:
                continue
            self._wait(e, tok)

    def _commit(self, tok, r, w):
        for b in r:
            b.rd.append(tok)
            if len(b.rd) > 64:
                b.rd = b.rd[-64:]
        for b in w:
            b.lw = tok
            b.rd = []

    def op(self, e, fn, r=(), w=()):
        w = list(w) + [b for b in r if b.psum and b not in w]
        self._deps(e, r, w)
        ins = fn(self.eng[e])
        self.cnt[e] += 1
        ins.then_inc(self.sem[e], 1)
        self._commit(("e", e, self.cnt[e]), r, w)
        self.n_inst += 1

    def dma(self, q, out, in_, r=(), w=(), **kw):
        sem = None
        for b in list(w) + list(r):
            if b.dsem is not None:
                sem = b.dsem
                break
        assert sem is not None
        self._deps(q, r, w)
        ins = self.eng[q].dma_start(out=out, in_=in_, **kw)
        self.dcnt[sem] += 16
        ins.then_inc(self.dsem[sem], 16)
        self._commit(("d", sem, self.dcnt[sem]), r, w)
        self.n_inst += 1

    def barrier(self):
        for e in self.eng:
            for k in self.sem:
                if self.cnt[k] > 0 and k != e:
                    self._wait(e, ("e", k, self.cnt[k]))
            for k in range(len(self.dsem)):
                if self.dcnt[k] > 0:
                    self._wait(e, ("d", k, self.dcnt[k]))

    def mm(self, out, lhsT, rhs, start, stop, r, w):
        self.op("pe", lambda e: e.matmul(out, lhsT=lhsT, rhs=rhs, start=start, stop=stop), r, w)

    def tr(self, out, in_, ident, r, w):
        self.op("pe", lambda e: e.transpose(out=out, in_=in_, identity=ident), r, w)

    def act(self, out, in_, func, r, w, **kw):
        self.op("act", lambda e: e.activation(out=out, in_=in_, func=func, **kw), r, w)

    def tt(self, eng, out, in0, in1, op, r, w):
        self.op(eng, lambda e: e.tensor_tensor(out=out, in0=in0, in1=in1, op=op), r, w)

    def ts(self, eng, out, in0, s1, s2, op0, op1, r, w):
        if op1 is None:
            self.op(eng, lambda e: e.tensor_scalar(out=out, in0=in0, scalar1=s1, scalar2=None, op0=op0), r, w)
        else:
            self.op(eng, lambda e: e.tensor_scalar(out=out, in0=in0, scalar1=s1, scalar2=s2, op0=op0, op1=op1), r, w)

    def stt(self, out, in0, scalar, in1, op0, op1, r, w):
        self.op("dve", lambda e: e.scalar_tensor_tensor(out=out, in0=in0, scalar=scalar, in1=in1, op0=op0, op1=op1), r, w)

    def copy(self, eng, out, in_, r, w):
        if eng == "act":
            self.act(out, in_, AF.Copy, r, w)
        else:
            self.op(eng, lambda e: e.tensor_copy(out=out, in_=in_), r, w)


def bc(ap, shape):
    return ap.to_broadcast(list(shape))


def build_program(S, depth=DEPTH, dbg=(), phases=('P1','P2a','P2b','P2c','P3','P3b','P4','P5','P6','P7')):
    NT = S // 128
    TG = min(512, S)
    NG = S // TG
    TPG = TG // 128
    nc = bass.Bass("TRN2", target_bir_lowering=False)

    def din(name, shape, dt=F32):
        return nc.dram_tensor(name, list(shape), dt, kind="ExternalInput").ap()

    x_in = din("x", [S, D])
    norm1_g = din("norm1_g", [depth, D])
    w_in = din("w_in", [depth, D, IN_COLS])
    conv_w = din("conv_w", [depth, 128, 24 * 5])
    a_log = din("a_log", [depth, 16])
    dt_bias = din("dt_bias", [depth, 16])
    ret_gn_g = din("ret_gn_g", [depth, D])
    dn_norm_g = din("dn_norm_g", [depth, 128])
    w_proj_ret = din("w_proj_ret", [depth, D, D])
    w_proj_dn = din("w_proj_dn", [depth, D, D])
    w_out = din("w_out", [depth, D, D])
    norm2_g = din("norm2_g", [depth, D])
    w_ff1 = din("w_ff1", [depth, D, DFF])
    w_ff2 = din("w_ff2", [depth, DFF, D])
    final_g = din("final_g", [D])
    k_rot = din("k_rot", [2, 128, S])
    k_ret = din("k_ret", [5, 128, 4 * 128])
    k_dn = din("k_dn", [6, 128, 128])
    k_id = din("k_id", [128, 128])
    k_dnm = din("k_dnm", [14, 128, 128])
    out = nc.dram_tensor("out", [S, D], F32, kind="ExternalOutput").ap()

    with ExitStack() as es:
        c = Ctx(nc, es)

        def scr(name, shape, dt):
            kind = "ExternalOutput" if name in dbg else "Internal"
            return c.dram(name, shape, dt, kind=kind)

        xA = scr("xA", [S, D], F32)
        xM = scr("xM", [S, D], F32)
        rqT = scr("rqT", [4, 128, S], BF16)
        rkT = scr("rkT", [4, 128, S], BF16)
        rk_tm = scr("rk_tm", [S, 512], BF16)
        rv = scr("rv", [S, D], BF16)
        rg = scr("rg", [S, D], BF16)
        dqT = scr("dqT", [8, 128, S], BF16)
        dkT = scr("dkT", [8, 128, S], BF16)
        dk_tm = scr("dk_tm", [S, D], BF16)
        dv_tm = scr("dv_tm", [S, D], BF16)
        dz = scr("dz", [S, D], BF16)
        dbg_ = scr("dbg", [S, 32], F32)
        gatesT = scr("gatesT", [16, 128, S], BF16)
        yret = scr("yret", [S, D], BF16)
        o_f = scr("o_f", [S, D], F32)
        o_b = scr("o_b", [S, D], F32)

        def load_w(dst, b_dst, src, q="pool"):
            c.dma(q, dst, src, w=[b_dst])

        def rms_rstd(ss, b_ss, n):
            c.act(ss[:, 1:2], ss[:, 0:1], AF.Ln, [b_ss], [b_ss], scale=1.0 / n, bias=EPS)
            c.act(ss[:, 1:2], ss[:, 1:2], AF.Exp, [b_ss], [b_ss], scale=-0.5)

        for l in range(depth):
          try:
            x_src = x_in if l == 0 else xA
            last = (l == depth - 1)

            with c.open():
                hT = c.sb("hT", [128, 8, S], BF16)
                b_hT = [c.buf("hT%d" % t) for t in range(NT)]
                ident = c.sb("ident", [128, 128], BF16)
                identf = c.sb("identf", [128, 128], F32)
                b_id = c.buf("ident", dma=True)
                c.dma("sp", identf[:], k_id, w=[b_id])
                c.copy("dve", ident[:], identf[:], [b_id], [b_id])
                onesb = c.sb("onesb", [128, 128], BF16)
                b_ones = c.buf("ones")
                c.op("dve", lambda e: e.memset(onesb[:], 1.0), w=[b_ones])
                g1 = c.sb("g1", [128, D], F32)
                b_g1 = c.buf("g1", dma=True)
                c.dma("sp", g1[:], norm1_g[l].partition_broadcast(128), w=[b_g1])
                U = [c.sb("U%d" % i, [128, max(S, 4096) + 4], F32) for i in range(2)]
                b_U = [c.buf("U%d" % i) for i in range(2)]
                acc = [c.sb("acc%d" % i, [128, max(S, 4096)], F32) for i in range(2)]
                b_acc = [c.buf("acc%d" % i) for i in range(2)]
                for i in range(2):
                    c.op("dve", lambda e: e.memset(U[i][:], 0.0), w=[b_U[i]])
                xt = [acc[1][:, i * 1024:(i + 1) * 1024] for i in range(2)]
                b_xt = [c.buf("xt%d" % i, dma=True) for i in range(2)]
                _hbv = acc[1][:, 2048:3072].bitcast(BF16)
                junk = acc[1][:, 3072:3584].bitcast(BF16)
                b_junk = c.buf("junk")
                ss = [c.sb("ss%d" % i, [128, 2], F32) for i in range(2)]
                b_ss = [c.buf("ss%d" % i) for i in range(2)]
                hb = [_hbv[:, i * 1024:(i + 1) * 1024] for i in range(2)]
                b_hb = [c.buf("hb%d" % i) for i in range(2)]
                pT = [c.ps("pT%d" % i, [128, 8, 128], BF16) for i in range(1)] * 2
                b_pT = [c.buf("pT%d" % i) for i in range(1)] * 2
                for t in range(NT):
                    s = t % 2
                    c.dma("sp", xt[s][:], x_src[t * 128:(t + 1) * 128, :], w=[b_xt[s]])
                    c.act(junk[:], xt[s][:], AF.Square, [b_xt[s]], [b_junk, b_ss[s]], accum_out=ss[s][:, 0:1])
                    rms_rstd(ss[s], b_ss[s], D)
                    c.stt(hb[s][:], xt[s][:], ss[s][:, 1:2], g1[:], ALU.mult, ALU.mult,
                          [b_xt[s], b_ss[s], b_g1], [b_hb[s]])
                    for kc in range(8):
                        c.tr(pT[s][:, kc, :], hb[s][:, kc * 128:(kc + 1) * 128], ident[:], [b_hb[s], b_id], [b_pT[s]])
                    c.copy("act" if t % 2 else "dve", hT[:, :, t * 128:(t + 1) * 128], pT[s][:], [b_pT[s]], [b_hT[t]])
                c.barrier()

                P2on = 'P2' in phases
                wl = w_in[l].rearrange("(kc p) n -> p kc n", p=128)
                wfm = [c.sb("wfm%d" % i, [128, 8, 256], BF16) for i in range(2)]
                b_wfm = [c.buf("wfm%d" % i, dma=True) for i in range(2)]
                cw = c.sb("cw", [128, 24 * 5], F32)
                b_cw = c.buf("cw", dma=True)
                c.dma("sp", cw[:], conv_w[l], w=[b_cw])
                sqb = c.sb("sqb", [128, S], BF16)
                b_sqb = c.buf("sqb")
                stg = [c.sb("stg%d" % i, [128, S], BF16) for i in range(2)]
                b_stg = [c.buf("stg%d" % i, dma=True) for i in range(2)]
                rot = [c.sb("rot%d" % i, [128, 2, TG], F32) for i in range(2)]
                b_rot = [c.buf("rot%d" % i, dma=True) for i in range(2)]
                t1 = [c.sb("t1_%d" % i, [128, 2, TG], F32) for i in range(2)]
                b_t1 = [c.buf("t1_%d" % i) for i in range(2)]
                rtmp = [c.sb("rtmp%d" % i, [128, TG], F32) for i in range(2)]
                b_rtmp = [c.buf("rtmp%d" % i) for i in range(2)]
                tmst = [c.sb("tmst%d" % i, [128, TPG, 128], BF16) for i in range(2)]
                b_tmst = [c.buf("tmst%d" % i, dma=True) for i in range(2)]
                pA = [c.ps("pA%d" % i, [128, 2, 512], F32) for i in range(2)]
                b_pA = [c.buf("pA%d" % i) for i in range(2)]
                pN = [c.ps("pN%d" % i, [128, 512], F32) for i in range(1)]
                b_pN = [c.buf("pN%d" % i) for i in range(1)]
                pTm = [c.ps("pTm%d" % i, [128, 8, 128], BF16) for i in range(2)]
                b_pTm = [c.buf("pTm%d" % i) for i in range(2)]
                cnt = {"job": 0, "pa": 0, "rot": 0, "tm": 0, "rt": 0}

                def fm_matmul(ws, b_w, ntile, g):
                    p = cnt["pa"] % 2
                    cnt["pa"] += 1
                    for j in range(ntile):
                        for kc in range(8):
                            c.mm(pA[p][:, j, 0:TG], ws[:, kc, j * 128:(j + 1) * 128], hT[:, kc, g * TG:(g + 1) * TG],
                                 kc == 0, kc == 7, [b_w] + b_hT[g * TPG:(g + 1) * TPG], [b_pA[p]])
                    return p

                def fm_to_tm(st, b_st, dst, col0):
                    for g in range(NG):
                        p = cnt["tm"] % 2
                        cnt["tm"] += 1
                        for j in range(TPG):
                            t = g * TPG + j
                            c.tr(pTm[p][:, j, :], st[:, t * 128:(t + 1) * 128], ident[:], [b_st, b_id], [b_pTm[p]])
                        c.copy("act", tmst[p][:], pTm[p][:, 0:TPG, :], [b_pTm[p]], [b_tmst[p]])
                        c.dma("sp", dst[g * TG:(g + 1) * TG, col0:col0 + 128].rearrange("(j p) d -> p j d", p=128),
                              tmst[p][:], r=[b_tmst[p]])

                def next_w(ntile, cols):
                    s = cnt["job"] % 2
                    cnt["job"] += 1
                    off = 0
                    for (c0, n) in cols:
                        load_w(wfm[s][:, :, off:off + n], b_wfm[s], wl[:, :, c0:c0 + n])
                        off += n
                    return s

                for (cbase, dstT, tmdst) in ((C_RQ, rqT, None), (C_RK, rkT, rk_tm)):
                    if 'P2a' not in phases:
                        break
                    for h in range(4):
                        c0 = cbase + h * 128
                        s = next_w(2, [(c0, 128), (c0 + 64, 64), (c0, 64)])
                        so = cnt["rt"] % 2
                        cnt["rt"] += 1
                        for g in range(NG):
                            rs = cnt["rot"] % 2
                            cnt["rot"] += 1
                            c.dma("sp", rot[rs][:], k_rot[:, :, g * TG:(g + 1) * TG].rearrange("a p s -> p a s"),
                                  w=[b_rot[rs]])
                            p = fm_matmul(wfm[s], b_wfm[s], 2, g)
                            c.tt("dve", t1[rs][:], pA[p][:, :, 0:TG], rot[rs][:], ALU.mult,
                                 [b_pA[p], b_rot[rs]], [b_t1[rs]])
                            c.tt("pool", stg[so][:, g * TG:(g + 1) * TG], t1[rs][:, 0, :], t1[rs][:, 1, :], ALU.add,
                                 [b_t1[rs]], [b_stg[so]])
                        c.dma("sp", dstT[h], stg[so][:], r=[b_stg[so]])
                        if tmdst is not None:
                            fm_to_tm(stg[so], b_stg[so], tmdst, h * 128)

                conv_jobs = []
                if 'P2b' in phases:
                    for (kind, cbase, dstT, tmdst) in (("q", C_DQ, dqT, None), ("k", C_DK, dkT, dk_tm), ("v", C_DV, None, dv_tm)):
                        for h in range(8):
                            conv_jobs.append((kind, cbase, dstT, tmdst, h))

                def conv_s1(j):
                    kind, cbase, dstT, tmdst, h = conv_jobs[j]
                    u = j % 2
                    s = next_w(1, [(cbase + h * 128, 128)])
                    for g in range(NG):
                        p = fm_matmul(wfm[s], b_wfm[s], 1, g)
                        c.copy("act", U[u][:, 2 + g * TG:2 + (g + 1) * TG], pA[p][:, 0, 0:TG], [b_pA[p]], [b_U[u]])

                def conv_s2(j):
                    kind, cbase, dstT, tmdst, h = conv_jobs[j]
                    u = j % 2
                    Uu, au, b_Uu, b_au = U[u], acc[u][:, 0:S], b_U[u], b_acc[u]
                    ti = {"q": 0, "k": 8, "v": 16}[kind] + h
                    so = cnt["rt"] % 2
                    cnt["rt"] += 1
                    c.ts("dve", au, Uu[:, 0:S], cw[:, ti * 5:ti * 5 + 1], None, ALU.mult, None, [b_Uu, b_cw], [b_au])
                    for k in range(1, 5):
                        c.stt(au, Uu[:, k:k + S], cw[:, ti * 5 + k:ti * 5 + k + 1], au, ALU.mult, ALU.add,
                              [b_Uu, b_cw, b_au], [b_au])
                    if kind == "v":
                        c.act(stg[so][:], au, AF.Silu, [b_au], [b_stg[so]])
                    else:
                        c.act(au, au, AF.Silu, [b_au], [b_au])
                        c.act(sqb[:], au, AF.Square, [b_au], [b_sqb])
                        for g in range(NG):
                            r2 = cnt["rot"] % 2
                            cnt["rot"] += 1
                            c.mm(pN[0][:, 0:TG], onesb[:], sqb[:, g * TG:(g + 1) * TG], True, True,
                                 [b_ones, b_sqb], [b_pN[0]])
                            c.act(rtmp[r2][:], pN[0][:, 0:TG], AF.Ln, [b_pN[0]], [b_rtmp[r2]], bias=EPS)
                            c.act(rtmp[r2][:], rtmp[r2][:], AF.Exp, [b_rtmp[r2]], [b_rtmp[r2]], scale=-0.5)
                            c.stt(stg[so][:, g * TG:(g + 1) * TG], au[:, g * TG:(g + 1) * TG],
                                  (128.0 ** -0.5) if kind == "q" else 1.0, rtmp[r2][:], ALU.mult, ALU.mult,
                                  [b_au, b_rtmp[r2]], [b_stg[so]])
                    if dstT is not None:
                        c.dma("sp", dstT[h], stg[so][:], r=[b_stg[so]])
                    if tmdst is not None:
                        fm_to_tm(stg[so], b_stg[so], tmdst, h * 128)

                if conv_jobs:
                    conv_s1(0)
                    for j in range(len(conv_jobs)):
                        if j + 1 < len(conv_jobs):
                            conv_s1(j + 1)
                        conv_s2(j)

                for f2 in range(8):
                    if 'P2c' not in phases:
                        break
                    s = next_w(2, [(C_GATE + f2 * 256, 256)])
                    sos = []
                    for j in range(2):
                        sos.append(cnt["rt"] % 2)
                        cnt["rt"] += 1
                    for g in range(NG):
                        p = fm_matmul(wfm[s], b_wfm[s], 2, g)
                        for j in range(2):
                            c.act(stg[sos[j]][:, g * TG:(g + 1) * TG], pA[p][:, j, 0:TG], AF.Sigmoid,
                                  [b_pA[p]], [b_stg[sos[j]]])
                    for j in range(2):
                        c.dma("sp", gatesT[f2 * 2 + j], stg[sos[j]][:], r=[b_stg[sos[j]]])

                c.barrier()
                _wv = U[1][:, 0:4096].bitcast(BF16)
                wtm = [_wv[:, i * 4096:(i + 1) * 4096].rearrange("p (k n) -> p k n", k=8) for i in range(2)]
                b_wtm = [c.buf("wtm%d" % i, dma=True) for i in range(2)]
                ost = [c.sb("ost%d" % i, [128, 512], BF16) for i in range(3)]
                b_ost = [c.buf("ost%d" % i, dma=True) for i in range(3)]
                alt = c.sb("alt", [128, 2, 16], F32)
                b_alt = c.buf("alt", dma=True)
                c.dma("sp", alt[:, 0, :], dt_bias[l].partition_broadcast(128), w=[b_alt])
                c.dma("sp", alt[:, 1, :], a_log[l].partition_broadcast(128), w=[b_alt])
                c.act(alt[:, 1, :], alt[:, 1, :], AF.Exp, [b_alt], [b_alt])
                c.ts("dve", alt[:, 1, :], alt[:, 1, :], -1.0, None, ALU.mult, None, [b_alt], [b_alt])
                bgs = [c.sb("bgs%d" % i, [128, 32], F32) for i in range(2)]
                b_bgs = [c.buf("bgs%d" % i, dma=True) for i in range(2)]
                k3 = 0
                banks = [(C_RV, rv, 0, AF.Copy), (C_RV + 512, rv, 512, AF.Copy),
                         (C_RG, rg, 0, AF.Silu), (C_RG + 512, rg, 512, AF.Silu),
                         (C_DZ, dz, 0, AF.Silu), (C_DZ + 512, dz, 512, AF.Silu)]
                for bi, (c0, dst, dc0, fn) in enumerate(banks):
                    if 'P3' not in phases:
                        break
                    s = bi % 2
                    load_w(wtm[s][:], b_wtm[s], wl[:, :, c0:c0 + 512])
                    for t in range(NT):
                        p = cnt["pa"] % 2
                        cnt["pa"] += 1
                        for kc in range(8):
                            c.mm(pA[p][:, 0, :], hT[:, kc, t * 128:(t + 1) * 128], wtm[s][:, kc, :], kc == 0, kc == 7,
                                 [b_hT[t], b_wtm[s]], [b_pA[p]])
                        o3 = k3 % 3
                        k3 += 1
                        c.act(ost[o3][:], pA[p][:, 0, :], fn, [b_pA[p]], [b_ost[o3]])
                        c.dma("sp", dst[t * 128:(t + 1) * 128, dc0:dc0 + 512], ost[o3][:], r=[b_ost[o3]])
                s = len(banks) % 2
                load_w(wtm[s][:, :, 0:32], b_wtm[s], wl[:, :, C_DB:C_DB + 32])
                for t in range(NT):
                    if 'P3b' not in phases:
                        break
                    p = cnt["pa"] % 2
                    cnt["pa"] += 1
                    for kc in range(8):
                        c.mm(pA[p][:, 0, 0:32], hT[:, kc, t * 128:(t + 1) * 128], wtm[s][:, kc, 0:32], kc == 0, kc == 7,
                             [b_hT[t], b_wtm[s]], [b_pA[p]])
                    q = t % 2
                    c.act(bgs[q][:, 0:16], pA[p][:, 0, 0:16], AF.Sigmoid, [b_pA[p]], [b_bgs[q]])
                    c.tt("dve", bgs[q][:, 16:32], pA[p][:, 0, 16:32], alt[:, 0, :], ALU.add, [b_pA[p], b_alt], [b_bgs[q]])
                    c.act(bgs[q][:, 16:32], bgs[q][:, 16:32], AF.Exp, [b_bgs[q]], [b_bgs[q]])
                    c.act(bgs[q][:, 16:32], bgs[q][:, 16:32], AF.Ln, [b_bgs[q]], [b_bgs[q]], bias=1.0)
                    c.tt("dve", bgs[q][:, 16:32], bgs[q][:, 16:32], alt[:, 1, :], ALU.mult, [b_bgs[q], b_alt], [b_bgs[q]])
                    c.dma("sp", dbg_[t * 128:(t + 1) * 128, :], bgs[q][:], r=[b_bgs[q]])
                c.barrier()

            with c.open():
                if 'P4' not in phases:
                    raise _Stop()
                gam = [1.0 - 2.0 ** (-5.0 - h) for h in range(4)]
                cdr = [float(np.float32(g_) ** 128) for g_ in gam]
                tab = c.sb("rtab", [128, 5, 512], F32)
                b_tab = c.buf("rtab", dma=True)
                c.dma("sp", tab[:], k_ret.rearrange("a p n -> p a n"), w=[b_tab])
                gng = c.sb("gng", [128, D], F32)
                b_gng = c.buf("gng", dma=True)
                c.dma("sp", gng[:], ret_gn_g[l].partition_broadcast(128), w=[b_gng])
                hist = c.sb("hist", [128, NT, 1024], BF16)
                b_hist = [c.buf("hist%d" % n) for n in range(NT)]
                Sb32 = c.sb("Sb32", [128, 1024], F32)
                Sf32 = c.sb("Sf32", [128, 1024], F32)
                b_Sb = c.buf("Sb32")
                b_Sf = c.buf("Sf32")
                c.op("dve", lambda e: e.memset(Sb32[:], 0.0), w=[b_Sb])
                c.op("dve", lambda e: e.memset(Sf32[:], 0.0), w=[b_Sf])
                Sfb = c.sb("Sfb", [128, 1024], BF16)
                b_Sfb = c.buf("Sfb")
                ktm = [c.sb("ktm%d" % i, [128, 512], BF16) for i in range(2)]
                b_ktm = [c.buf("ktm%d" % i, dma=True) for i in range(2)]
                vt = [c.sb("vt%d" % i, [128, D], BF16) for i in range(2)]
                b_vt = [c.buf("vt%d" % i, dma=True) for i in range(2)]
                kx = [c.sb("kx%d" % i, [128, 512], BF16) for i in range(2)]
                b_kx = [c.buf("kx%d" % i) for i in range(2)]
                pKV = [c.ps("pKV%d" % i, [128, 1024], F32) for i in range(1)]
                b_pKV = [c.buf("pKV%d" % i) for i in range(1)]
                for i, n in enumerate(range(NT - 1, -1, -1)):
                    s = i % 2
                    c.dma("sp", ktm[s][:], rk_tm[n * 128:(n + 1) * 128, :], w=[b_ktm[s]])
                    c.dma("sp", vt[s][:], rv[n * 128:(n + 1) * 128, :], w=[b_vt[s]])
                    c.copy("act", hist[:, n, :], Sb32[:], [b_Sb], [b_hist[n]])
                    c.tt("pool", kx[s][:], ktm[s][:], tab[:, 2, :], ALU.mult, [b_ktm[s], b_tab], [b_kx[s]])
                    for h in range(4):
                        c.mm(pKV[0][:, h * 256:(h + 1) * 256], kx[s][:, h * 128:(h + 1) * 128], vt[s][:, h * 256:(h + 1) * 256],
                             True, True, [b_kx[s], b_vt[s]], [b_pKV[0]])
                    for h in range(4):
                        c.stt(Sb32[:, h * 256:(h + 1) * 256], Sb32[:, h * 256:(h + 1) * 256], cdr[h],
                              pKV[0][:, h * 256:(h + 1) * 256], ALU.mult, ALU.add, [b_Sb, b_pKV[0]], [b_Sb])
                qT = [c.sb("qT%d" % i, [128, 4, 128], BF16) for i in range(2)]
                b_qT = [c.buf("qT%d" % i, dma=True) for i in range(2)]
                kT = [c.sb("kT%d" % i, [128, 4, 128], BF16) for i in range(2)]
                b_kT = [c.buf("kT%d" % i, dma=True) for i in range(2)]
                rgt = [c.sb("rgt%d" % i, [128, D], BF16) for i in range(2)]
                b_rgt = [c.buf("rgt%d" % i, dma=True) for i in range(2)]
                sm = [c.sb("sm%d" % i, [128, 512], BF16) for i in range(2)]
                b_sm = [c.buf("sm%d" % i) for i in range(2)]
                qf = [c.sb("qf%d" % i, [128, 512], BF16) for i in range(2)]
                b_qf = [c.buf("qf%d" % i) for i in range(2)]
                qb = [c.sb("qb%d" % i, [128, 512], BF16) for i in range(2)]
                b_qb = [c.buf("qb%d" % i) for i in range(2)]
                st6 = [c.sb("st6_%d" % i, [128, 4, 6], F32) for i in range(2)]
                b_st6 = [c.buf("st6_%d" % i) for i in range(2)]
                mv = [c.sb("mv%d" % i, [128, 4, 2], F32) for i in range(2)]
                b_mv = [c.buf("mv%d" % i) for i in range(2)]
                rs4 = [c.sb("rs4_%d" % i, [128, 4], F32) for i in range(2)]
                b_rs4 = [c.buf("rs4_%d" % i) for i in range(2)]
                yn = [c.sb("yn%d" % i, [128, D], F32) for i in range(2)]
                b_yn = [c.buf("yn%d" % i) for i in range(2)]
                gs = [c.sb("gs%d" % i, [128, D], F32) for i in range(2)]
                b_gs = [c.buf("gs%d" % i) for i in range(2)]
                yo = [c.sb("yo%d" % i, [128, D], BF16) for i in range(2)]
                b_yo = [c.buf("yo%d" % i, dma=True) for i in range(2)]
                pSc = [c.ps("pSc%d" % i, [128, 512], F32) for i in range(1)]
                b_pSc = [c.buf("pSc%d" % i) for i in range(1)]
                pO = [c.ps("pO%d" % i, [128, 1024], F32) for i in range(2)]
                b_pO = [c.buf("pO%d" % i) for i in range(2)]
                rq_v = rqT.rearrange("h p s -> p h s")
                rk_v = rkT.rearrange("h p s -> p h s")
                for n in range(NT):
                    s = n % 2
                    tk = slice(n * 128, (n + 1) * 128)
                    c.dma("sp", qT[s][:], rq_v[:, :, tk], w=[b_qT[s]])
                    c.dma("sp", kT[s][:], rk_v[:, :, tk], w=[b_kT[s]])
                    c.dma("sp", ktm[s][:], rk_tm[tk, :], w=[b_ktm[s]])
                    c.dma("sp", vt[s][:], rv[tk, :], w=[b_vt[s]])
                    c.dma("sp", rgt[s][:], rg[tk, :], w=[b_rgt[s]])
                    c.copy("act", Sfb[:], Sf32[:], [b_Sf], [b_Sfb])
                    for h in range(4):
                        c.mm(pSc[0][:, h * 128:(h + 1) * 128], kT[s][:, h, :], qT[s][:, h, :], True, True,
                             [b_kT[s], b_qT[s]], [b_pSc[0]])
                    c.tt("dve", sm[s][:], pSc[0][:], tab[:, 0, :], ALU.mult, [b_pSc[0], b_tab], [b_sm[s]])
                    qflat = qT[s][:].rearrange("p h d -> p (h d)")
                    c.tt("pool", qf[s][:], qflat, tab[:, 3, :], ALU.mult, [b_qT[s], b_tab], [b_qf[s]])
                    c.tt("pool", qb[s][:], qflat, tab[:, 4, :], ALU.mult, [b_qT[s], b_tab], [b_qb[s]])
                    c.tt("pool", kx[s][:], ktm[s][:], tab[:, 1, :], ALU.mult, [b_ktm[s], b_tab], [b_kx[s]])
                    for h in range(4):
                        vs = slice(h * 256, (h + 1) * 256)
                        hs = slice(h * 128, (h + 1) * 128)
                        c.mm(pO[s][:, vs], sm[s][:, hs], vt[s][:, vs], True, False, [b_sm[s], b_vt[s]], [b_pO[s]])
                        c.mm(pO[s][:, vs], qf[s][:, hs], Sfb[:, vs], False, False, [b_qf[s], b_Sfb], [b_pO[s]])
                        c.mm(pO[s][:, vs], qb[s][:, hs], hist[:, n, vs], False, True, [b_qb[s], b_hist[n]], [b_pO[s]])
                    for h in range(4):
                        c.mm(pKV[0][:, h * 256:(h + 1) * 256], kx[s][:, h * 128:(h + 1) * 128], vt[s][:, h * 256:(h + 1) * 256],
                             True, True, [b_kx[s], b_vt[s]], [b_pKV[0]])
                    for h in range(4):
                        c.stt(Sf32[:, h * 256:(h + 1) * 256], Sf32[:, h * 256:(h + 1) * 256], cdr[h],
                              pKV[0][:, h * 256:(h + 1) * 256], ALU.mult, ALU.add, [b_Sf, b_pKV[0]], [b_Sf])
                    for h in range(4):
                        c.op("dve", lambda e: e.bn_stats(out=st6[s][:, h, :], in_=pO[s][:, h * 256:(h + 1) * 256]),
                             [b_pO[s]], [b_st6[s]])
                    for h in range(4):
                        c.op("dve", lambda e: e.bn_aggr(out=mv[s][:, h, :], in_=st6[s][:, h, :]), [b_st6[s]], [b_mv[s]])
                    c.act(rs4[s][:], mv[s][:, :, 1], AF.Ln, [b_mv[s]], [b_rs4[s]], bias=EPS)
                    c.act(rs4[s][:], rs4[s][:], AF.Exp, [b_rs4[s]], [b_rs4[s]], scale=-0.5)
                    for h in range(4):
                        c.ts("dve", yn[s][:, h * 256:(h + 1) * 256], pO[s][:, h * 256:(h + 1) * 256],
                             mv[s][:, h, 0:1], rs4[s][:, h:h + 1], ALU.subtract, ALU.mult,
                             [b_pO[s], b_mv[s], b_rs4[s]], [b_yn[s]])
                    c.tt("pool", gs[s][:], rgt[s][:], gng[:], ALU.mult, [b_rgt[s], b_gng], [b_gs[s]])
                    c.tt("pool", yo[s][:], yn[s][:], gs[s][:], ALU.mult, [b_yn[s], b_gs[s]], [b_yo[s]])
                    c.dma("sp", yret[tk, :], yo[s][:], r=[b_yo[s]])
                c.barrier()

            with c.open():
                if 'P5' not in phases:
                    raise _Stop()
                ident = c.sb("ident", [128, 128], BF16)
                identf = c.sb("identf", [128, 128], F32)
                b_id = c.buf("ident", dma=True)
                c.dma("sp", identf[:], k_id, w=[b_id])
                c.copy("dve", ident[:], identf[:], [b_id], [b_id])
                onesf = c.sb("onesf", [128, 128], F32)
                b_ones = c.buf("onesf")
                c.op("dve", lambda e: e.memset(onesf[:], 1.0), w=[b_ones])
                dtab = c.sb("dtab", [128, 6, 128], F32)
                b_dtab = c.buf("dtab", dma=True)
                c.dma("sp", dtab[:], k_dn.rearrange("a p n -> p a n"), w=[b_dtab])
                mtab = c.sb("mtab", [128, 14, 128], F32)
                b_mtab = c.buf("mtab", dma=True)
                c.dma("sp", mtab[:], k_dnm.rearrange("a p n -> p a n"), w=[b_mtab])
                S32 = [c.sb("S32_%d" % i, [128, 8, 128], F32) for i in range(2)]
                Sbf = [c.sb("Sbf_%d" % i, [128, 8, 128], BF16) for i in range(2)]
                b_S32 = [[c.buf("S32_%d_%d" % (i, hf)) for hf in range(2)] for i in range(2)]
                b_Sbf = [[c.buf("Sbf_%d_%d" % (i, hf)) for hf in range(2)] for i in range(2)]
                for i in range(2):
                    c.op("dve", lambda e: e.memset(S32[i][:], 0.0), w=b_S32[i])
                    c.op("dve", lambda e: e.memset(Sbf[i][:], 0.0), w=b_Sbf[i])

                def two(name, shape, dt, dma=False):
                    return ([c.sb("%s%d" % (name, i), shape, dt) for i in range(2)],
                            [c.buf("%s%d" % (name, i), dma=dma) for i in range(2)])

                def twoh(name, shape, dt):
                    return ([c.sb("%s%d" % (name, i), shape, dt) for i in range(2)],
                            [[c.buf("%s%d_%d" % (name, i, hf)) for hf in range(2)] for i in range(2)])
                qT, b_qT = two("dqT", [128, 8, 128], BF16, True)
                kT, b_kT = two("dkT", [128, 8, 128], BF16, True)
                ktm, b_ktm = two("dktm", [128, 8, 128], BF16, True)
                vtm, b_vtm = two("dvtm", [128, 8, 128], BF16, True)
                bg, b_bg = two("bg", [128, 32], F32, True)
                rhsG, b_rhsG = twoh("rhsG", [128, 8, 128], F32)
                sc, b_sc = twoh("sc", [128, 8, 8], F32)
                X, b_X = twoh("X", [128, 8, 128], F32)
                E, b_E = twoh("E", [128, 8, 128], F32)
                E3, b_E3 = twoh("E3", [128, 8, 128], F32)
                EG, b_EG = twoh("EG", [128, 8, 128], F32)
                P0, b_P0 = twoh("P0", [128, 8, 128], BF16)
                attn, b_attn = twoh("attn", [128, 8, 128], BF16)
                QA, b_QA = twoh("QA", [128, 16, 128], BF16)
                Tb, b_Tb = twoh("Tb", [128, 8, 128], BF16)
                Ttb, b_Ttb = twoh("Ttb", [128, 8, 128], BF16)
                X1m, b_X1m = twoh("X1m", [128, 8, 128], BF16)
                bv, b_bv = twoh("bv", [128, 8, 128], BF16)
                bk, b_bk = twoh("bk", [128, 8, 128], BF16)
                usb, b_usb = twoh("usb", [128, 8, 128], F32)
                wTn, b_wTn = twoh("wTn", [128, 8, 128], BF16)
                qdT, b_qdT = twoh("qdT", [128, 8, 128], BF16)
                kd, b_kd = twoh("kd", [128, 8, 128], BF16)
                dl, b_dl = twoh("dl", [128, 8, 128], BF16)
                osb, b_osb = two("osb", [128, 1024], F32, True)
                pG = c.ps("pG", [128, 8, 128], F32)
                pK = c.ps("pK", [128, 8, 128], F32)
                pQ = c.ps("pQ", [128, 8, 128], F32)
                pX = c.ps("pX", [128, 16, 128], BF16)
                b_pG = [c.buf("pG%d" % i) for i in range(2)]
                b_pK = [c.buf("pK%d" % i) for i in range(2)]
                b_pQ = [c.buf("pQ%d" % i) for i in range(2)]
                b_pX = [c.buf("pX%d" % i) for i in range(2)]
                dq_v = dqT.rearrange("h p s -> p h s")
                dk_v = dkT.rearrange("h p s -> p h s")
                HF = (0, 1)
                it = 0
                for t in range(NT):
                    for dr in range(2):
                        n = t if dr == 0 else NT - 1 - t
                        s = it % 2
                        it += 1
                        tk = slice(n * 128, (n + 1) * 128)
                        last_i = 127 if dr == 0 else 0
                        tri = dtab[:, 0 + dr, :]
                        lim = dtab[:, 2 + dr, :]
                        strict = dtab[:, 4 + dr, :]
                        c.dma("sp", qT[s][:], dq_v[:, :, tk], w=[b_qT[s]])
                        c.dma("sp", kT[s][:], dk_v[:, :, tk], w=[b_kT[s]])
                        c.dma("sp", ktm[s][:], dk_tm[tk, :].rearrange("p (h d) -> p h d", h=8), w=[b_ktm[s]])
                        c.dma("sp", vtm[s][:], dv_tm[tk, :].rearrange("p (h d) -> p h d", h=8), w=[b_vtm[s]])
                        c.dma("sp", bg[s][:], dbg_[tk, :], w=[b_bg[s]])
                        H = [slice(hf * 4, (hf + 1) * 4) for hf in HF]
                        beta = [bg[s][:, dr * 8 + hf * 4:dr * 8 + hf * 4 + 4] for hf in HF]
                        gl = [bg[s][:, 16 + dr * 8 + hf * 4:16 + dr * 8 + hf * 4 + 4] for hf in HF]
                        SH = [128, 4, 128]
                        for hf in HF:
                            c.tt("dve", rhsG[s][:, H[hf], :], bc(tri.unsqueeze(1), SH), bc(gl[hf].unsqueeze(2), SH),
                                 ALU.mult, [b_dtab, b_bg[s]], [b_rhsG[s][hf]])
                        for hf in HF:
                            c.mm(pG[:, H[hf], :], onesf[:], rhsG[s][:, H[hf], :], True, True,
                                 [b_ones, b_rhsG[s][hf]], [b_pG[hf]])
                            c.mm(pQ[:, hf * 4, 0:4], tri, gl[hf], True, True, [b_dtab, b_bg[s]], [b_pQ[hf]])
                        for hf in HF:
                            c.copy("act", sc[s][:, H[hf], 0], pQ[:, hf * 4, 0:4], [b_pQ[hf]], [b_sc[s][hf]])
                        for hf in HF:
                            gcs = sc[s][:, H[hf], 0]
                            c.tt("dve", X[s][:, H[hf], :], bc(gcs.unsqueeze(2), SH), pG[:, H[hf], :], ALU.subtract,
                                 [b_sc[s][hf], b_pG[hf]], [b_X[s][hf]])
                            c.tt("pool", X[s][:, H[hf], :], X[s][:, H[hf], :], bc(lim.unsqueeze(1), SH), ALU.add,
                                 [b_X[s][hf], b_dtab], [b_X[s][hf]])
                        for hf in HF:
                            gcs = sc[s][:, H[hf], 0]
                            c.act(E[s][:, H[hf], :], X[s][:, H[hf], :], AF.Exp, [b_X[s][hf]], [b_E[s][hf]])
                            c.act(EG[s][:, H[hf], :], pG[:, H[hf], :], AF.Exp, [b_pG[hf]], [b_EG[s][hf]])
                            c.act(sc[s][:, H[hf], 1], gcs, AF.Exp, [b_sc[s][hf]], [b_sc[s][hf]])
                            c.act(sc[s][:, H[hf], 5], pG[:, H[hf], last_i], AF.Exp, [b_pG[hf]], [b_sc[s][hf]])
                            c.tt("dve", sc[s][:, H[hf], 4], pG[:, H[hf], last_i], gcs, ALU.subtract,
                                 [b_pG[hf], b_sc[s][hf]], [b_sc[s][hf]])
                            c.act(sc[s][:, H[hf], 4], sc[s][:, H[hf], 4], AF.Exp, [b_sc[s][hf]], [b_sc[s][hf]])
                            c.tt("pool", sc[s][:, H[hf], 2], sc[s][:, H[hf], 1], beta[hf], ALU.mult,
                                 [b_sc[s][hf], b_bg[s]], [b_sc[s][hf]])
                            c.ts("pool", sc[s][:, H[hf], 3], beta[hf], -1.0, 0.0, ALU.mult, ALU.add, [b_bg[s]], [b_sc[s][hf]])
                        for hf in HF:
                            for hh in range(4):
                                h = hf * 4 + hh
                                c.mm(pK[:, h, :], kT[s][:, h, :], kT[s][:, h, :], True, True, [b_kT[s]], [b_pK[hf]])
                            for hh in range(4):
                                h = hf * 4 + hh
                                c.mm(pQ[:, h, :], qT[s][:, h, :], kT[s][:, h, :], True, True, [b_qT[s], b_kT[s]], [b_pQ[hf]])
                        for hf in HF:
                            c.tt("pool", E3[s][:, H[hf], :], E[s][:, H[hf], :], bc(strict.unsqueeze(1), SH), ALU.mult,
                                 [b_E[s][hf], b_dtab], [b_E3[s][hf]])
                            c.tt("pool", E3[s][:, H[hf], :], E3[s][:, H[hf], :], bc(sc[s][:, H[hf], 3].unsqueeze(2), SH), ALU.mult,
                                 [b_E3[s][hf], b_sc[s][hf]], [b_E3[s][hf]])
                        for hf in HF:
                            c.tt("dve", P0[s][:, H[hf], :], pK[:, H[hf], :], E3[s][:, H[hf], :], ALU.mult,
                                 [b_pK[hf], b_E3[s][hf]], [b_P0[s][hf]])
                            c.tt("dve", attn[s][:, H[hf], :], pQ[:, H[hf], :], E[s][:, H[hf], :], ALU.mult,
                                 [b_pQ[hf], b_E[s][hf]], [b_attn[s][hf]])
                        for hf in HF:
                            for hh in range(4):
                                c.tr(pX[:, hf * 8 + hh, :], P0[s][:, hf * 4 + hh, :], ident[:], [b_P0[s][hf], b_id], [b_pX[hf]])
                            for hh in range(4):
                                c.tr(pX[:, hf * 8 + 4 + hh, :], attn[s][:, hf * 4 + hh, :], ident[:], [b_attn[s][hf], b_id], [b_pX[hf]])
                        for hf in HF:
                            c.copy("act", QA[s][:, hf * 8:(hf + 1) * 8, :], pX[:, hf * 8:(hf + 1) * 8, :], [b_pX[hf]], [b_QA[s][hf]])
                        m_dir, m_opp = dr * 7, (1 - dr) * 7
                        for hf in HF:
                            Q0h = QA[s][:, hf * 8:hf * 8 + 4, :]
                            c.tt("pool", Tb[0][:, H[hf], :], P0[s][:, H[hf], :], bc(mtab[:, m_dir, :].unsqueeze(1), SH), ALU.mult,
                                 [b_P0[s][hf], b_mtab], [b_Tb[0][hf]])
                            c.tt("pool", Tb[0][:, H[hf], :], Tb[0][:, H[hf], :], bc(identf[:].unsqueeze(1), SH), ALU.add,
                                 [b_Tb[0][hf], b_id], [b_Tb[0][hf]])
                            c.tt("pool", Ttb[0][:, H[hf], :], Q0h, bc(mtab[:, m_opp, :].unsqueeze(1), SH), ALU.mult,
                                 [b_QA[s][hf], b_mtab], [b_Ttb[0][hf]])
                            c.tt("pool", Ttb[0][:, H[hf], :], Ttb[0][:, H[hf], :], bc(identf[:].unsqueeze(1), SH), ALU.add,
                                 [b_Ttb[0][hf], b_id], [b_Ttb[0][hf]])
                        for lev in range(1, 7):
                            cu, nx = (lev - 1) % 2, lev % 2
                            x1 = lev % 2
                            for hf in HF:
                                for hh in range(4):
                                    h = hf * 4 + hh
                                    c.mm(pG[:, h, :], QA[s][:, hf * 8 + hh, :], Tb[cu][:, h, :], True, True,
                                         [b_QA[s][hf], b_Tb[cu][hf]], [b_pG[hf]])
                            for hf in HF:
                                c.tt("dve", X1m[x1][:, H[hf], :], pG[:, H[hf], :], bc(mtab[:, m_dir + lev, :].unsqueeze(1), SH), ALU.mult,
                                     [b_pG[hf], b_mtab], [b_X1m[x1][hf]])
                            for hf in HF:
                                for hh in range(4):
                                    h = hf * 4 + hh
                                    if lev < 6:
                                        c.mm(pK[:, h, :], Ttb[cu][:, h, :], X1m[x1][:, h, :], True, False,
                                             [b_Ttb[cu][hf], b_X1m[x1][hf]], [b_pK[hf]])
                                        c.mm(pK[:, h, :], ident[:], Tb[cu][:, h, :], False, True,
                                             [b_id, b_Tb[cu][hf]], [b_pK[hf]])
                                    c.mm(pQ[:, h, :], X1m[x1][:, h, :], Ttb[cu][:, h, :], True, False,
                                         [b_Ttb[cu][hf], b_X1m[x1][hf]], [b_pQ[hf]])
                                    c.mm(pQ[:, h, :], ident[:], Ttb[cu][:, h, :], False, True,
                                         [b_id, b_Ttb[cu][hf]], [b_pQ[hf]])
                            for hf in HF:
                                if lev < 6:
                                    c.copy("act", Tb[nx][:, H[hf], :], pK[:, H[hf], :], [b_pK[hf]], [b_Tb[nx][hf]])
                                c.copy("act", Ttb[nx][:, H[hf], :], pQ[:, H[hf], :], [b_pQ[hf]], [b_Ttb[nx][hf]])
                        Rf, b_Rf = Ttb[0], b_Ttb[0]
                        for hf in HF:
                            c.tt("pool", bv[s][:, H[hf], :], vtm[s][:, H[hf], :], bc(beta[hf].unsqueeze(2), SH), ALU.mult,
                                 [b_vtm[s], b_bg[s]], [b_bv[s][hf]])
                            c.tt("pool", bk[s][:, H[hf], :], ktm[s][:, H[hf], :], bc(sc[s][:, H[hf], 2].unsqueeze(2), SH), ALU.mult,
                                 [b_ktm[s], b_sc[s][hf]], [b_bk[s][hf]])
                            c.tt("pool", qdT[s][:, H[hf], :], qT[s][:, H[hf], :], EG[s][:, H[hf], :], ALU.mult,
                                 [b_qT[s], b_EG[s][hf]], [b_qdT[s][hf]])
                            c.tt("pool", kd[s][:, H[hf], :], ktm[s][:, H[hf], :], bc(sc[s][:, H[hf], 4].unsqueeze(2), SH), ALU.mult,
                                 [b_ktm[s], b_sc[s][hf]], [b_kd[s][hf]])
                        for hf in HF:
                            for hh in range(4):
                                h = hf * 4 + hh
                                c.mm(pG[:, h, :], Rf[:, h, :], bv[s][:, h, :], True, True, [b_Rf[hf], b_bv[s][hf]], [b_pG[hf]])
                            for hh in range(4):
                                h = hf * 4 + hh
                                c.mm(pK[:, h, :], bk[s][:, h, :], Rf[:, h, :], True, True, [b_Rf[hf], b_bk[s][hf]], [b_pK[hf]])
                        for hf in HF:
                            c.copy("act", usb[s][:, H[hf], :], pG[:, H[hf], :], [b_pG[hf]], [b_usb[s][hf]])
                            c.ts("dve", wTn[s][:, H[hf], :], pK[:, H[hf], :], -1.0, None, ALU.mult, None, [b_pK[hf]], [b_wTn[s][hf]])
                        for hf in HF:
                            for hh in range(4):
                                h = hf * 4 + hh
                                c.mm(pQ[:, h, :], wTn[s][:, h, :], Sbf[dr][:, h, :], True, True,
                                     [b_wTn[s][hf], b_Sbf[dr][hf]], [b_pQ[hf]])
                        for hf in HF:
                            c.tt("dve", dl[s][:, H[hf], :], usb[s][:, H[hf], :], pQ[:, H[hf], :], ALU.add,
                                 [b_usb[s][hf], b_pQ[hf]], [b_dl[s][hf]])
                        for hf in HF:
                            for hh in range(4):
                                h = hf * 4 + hh
                                c.mm(pG[:, h, :], qdT[s][:, h, :], Sbf[dr][:, h, :], True, False,
                                     [b_qdT[s][hf], b_Sbf[dr][hf]], [b_pG[hf]])
                                c.mm(pG[:, h, :], QA[s][:, hf * 8 + 4 + hh, :], dl[s][:, h, :], False, True,
                                     [b_QA[s][hf], b_dl[s][hf]], [b_pG[hf]])
                            for hh in range(4):
                                h = hf * 4 + hh
                                c.mm(pK[:, h, :], kd[s][:, h, :], dl[s][:, h, :], True, True, [b_kd[s][hf], b_dl[s][hf]], [b_pK[hf]])
                        for hf in HF:
                            c.copy("act", osb[s][:, hf * 512:(hf + 1) * 512], pG[:, H[hf], :].rearrange("p h d -> p (h d)"),
                                   [b_pG[hf]], [b_osb[s]])
                            c.tt("pool", S32[dr][:, H[hf], :], S32[dr][:, H[hf], :], bc(sc[s][:, H[hf], 5].unsqueeze(2), SH), ALU.mult,
                                 [b_S32[dr][hf], b_sc[s][hf]], [b_S32[dr][hf]])
                        c.dma("sp", (o_f if dr == 0 else o_b)[tk, :], osb[s][:], r=[b_osb[s]])
                        for hf in HF:
                            c.tt("dve", S32[dr][:, H[hf], :], S32[dr][:, H[hf], :], pK[:, H[hf], :], ALU.add,
                                 [b_S32[dr][hf], b_pK[hf]], [b_S32[dr][hf]])
                            c.copy("act", Sbf[dr][:, H[hf], :], S32[dr][:, H[hf], :], [b_S32[dr][hf]], [b_Sbf[dr][hf]])
                c.barrier()

            with c.open():
                if 'P6' not in phases:
                    raise _Stop()
                ident = c.sb("ident", [128, 128], BF16)
                identf = c.sb("identf", [128, 128], F32)
                b_id = c.buf("ident", dma=True)
                c.dma("sp", identf[:], k_id, w=[b_id])
                c.copy("dve", ident[:], identf[:], [b_id], [b_id])
                Wp = [c.sb("Wp%d" % i, [128, 8, D], BF16) for i in range(3)]
                b_Wp = [c.buf("Wp%d" % i, dma=True) for i in range(3)]
                for i, wsrc in enumerate((w_proj_ret, w_proj_dn, w_out)):
                    wv = wsrc[l].rearrange("(kc p) n -> p kc n", p=128)
                    for hh in range(2):
                        load_w(Wp[i][:, :, hh * 512:(hh + 1) * 512], b_Wp[i], wv[:, :, hh * 512:(hh + 1) * 512])
                dng = c.sb("dng", [128, 128], F32)
                b_dng = c.buf("dng", dma=True)
                c.dma("sp", dng[:], dn_norm_g[l].partition_broadcast(128), w=[b_dng])

                def two(name, shape, dt, dma=False):
                    return ([c.sb("%s%d" % (name, i), shape, dt) for i in range(2)],
                            [c.buf("%s%d" % (name, i), dma=dma) for i in range(2)])
                yr, b_yr = two("yr", [128, D], BF16, True)
                of_, b_of = two("of", [128, 8, 128], F32, True)
                ob_, b_ob = two("ob", [128, 8, 128], F32, True)
                dzt, b_dzt = two("dzt", [128, 8, 128], BF16, True)
                xt, b_xt = two("xt6", [128, D], F32, True)
                gt, b_gt = two("gt", [128, 16, 128], BF16, True)
                sq, b_sq = two("sq6", [128, 8, 128], F32)
                s8, b_s8 = two("s8", [128, 2, 8], F32)
                gz, b_gz = two("gz", [128, 8, 128], F32)
                yd, b_yd = two("yd", [128, 8, 128], BF16)
                yT, b_yT = two("yT", [128, 16, 128], BF16)
                m1, b_m1 = two("m1", [128, 16, 128], F32)
                mT, b_mT = two("mT", [128, 8, 128], BF16)
                xo, b_xo = two("xo", [128, D], F32, True)
                pY = c.ps("pY", [128, 16, 128], BF16)
                pP = c.ps("pP", [128, 16, 128], F32)
                pXo = c.ps("pXo", [128, D], F32)
                b_pY, b_pP, b_pXo = c.buf("pY"), c.buf("pP"), c.buf("pXo")
                b_pPh = [c.buf("pPh%d" % i) for i in range(2)]
                b_m1h = [[c.buf("m1h%d_%d" % (i, hb)) for hb in range(2)] for i in range(2)]
                g_v = gatesT.rearrange("f p s -> p f s")

                def p6_prep(n):
                    s = n % 2
                    tk = slice(n * 128, (n + 1) * 128)
                    c.dma("sp", yr[s][:], yret[tk, :], w=[b_yr[s]])
                    c.dma("sp", of_[s][:], o_f[tk, :].rearrange("p (h d) -> p h d", h=8), w=[b_of[s]])
                    c.dma("sp", ob_[s][:], o_b[tk, :].rearrange("p (h d) -> p h d", h=8), w=[b_ob[s]])
                    c.dma("sp", dzt[s][:], dz[tk, :].rearrange("p (h d) -> p h d", h=8), w=[b_dzt[s]])
                    c.dma("sp", xt[s][:], x_src[tk, :], w=[b_xt[s]])
                    c.dma("sp", gt[s][:], g_v[:, :, tk], w=[b_gt[s]])
                    c.tt("pool", of_[s][:], of_[s][:], ob_[s][:], ALU.add, [b_of[s], b_ob[s]], [b_of[s]])
                    c.act(sq[s][:], of_[s][:], AF.Square, [b_of[s]], [b_sq[s]])
                    c.op("dve", lambda e: e.tensor_reduce(out=s8[s][:, 0, :], in_=sq[s][:], axis=AX.X, op=ALU.add),
                         [b_sq[s]], [b_s8[s]])
                    c.act(s8[s][:, 1, :], s8[s][:, 0, :], AF.Ln, [b_s8[s]], [b_s8[s]], scale=1.0 / 128, bias=EPS)
                    c.act(s8[s][:, 1, :], s8[s][:, 1, :], AF.Exp, [b_s8[s]], [b_s8[s]], scale=-0.5)
                    c.tt("pool", gz[s][:], dzt[s][:], bc(dng[:].unsqueeze(1), [128, 8, 128]), ALU.mult,
                         [b_dzt[s], b_dng], [b_gz[s]])
                    c.tt("dve", sq[s][:], of_[s][:], bc(s8[s][:, 1, :].unsqueeze(2), [128, 8, 128]), ALU.mult,
                         [b_of[s], b_s8[s]], [b_sq[s]])
                    c.tt("pool", yd[s][:], sq[s][:], gz[s][:], ALU.mult, [b_sq[s], b_gz[s]], [b_yd[s]])
                    for kc in range(8):
                        c.tr(pY[:, kc, :], yr[s][:, kc * 128:(kc + 1) * 128], ident[:], [b_yr[s], b_id], [b_pY])
                    for kc in range(8):
                        c.tr(pY[:, 8 + kc, :], yd[s][:, kc, :], ident[:], [b_yd[s], b_id], [b_pY])
                    c.copy("act", yT[s][:], pY[:], [b_pY], [b_yT[s]])

                def p6_main(n):
                    s = n % 2
                    tk = slice(n * 128, (n + 1) * 128)
                    gtv = gt[s][:].rearrange("p (b d) t -> p b d t", b=2)
                    for hb in range(2):
                        for br in range(2):
                            for dd in range(4):
                                do = hb * 4 + dd
                                for kc in range(8):
                                    c.mm(pP[:, hb * 8 + br * 4 + dd, :], Wp[br][:, kc, do * 128:(do + 1) * 128],
                                         yT[s][:, br * 8 + kc, :], kc == 0, kc == 7, [b_Wp[br], b_yT[s]], [b_pPh[hb]])
                        hsl = slice(hb * 8, (hb + 1) * 8)
                        c.tt("dve", m1[s][:, hsl, :].rearrange("p (b d) t -> p b d t", b=2),
                             pP[:, hsl, :].rearrange("p (b d) t -> p b d t", b=2), gtv[:, :, hb * 4:(hb + 1) * 4, :],
                             ALU.mult, [b_pPh[hb], b_gt[s]], [b_m1h[s][hb]])
                        c.tt("pool", mT[s][:, hb * 4:(hb + 1) * 4, :], m1[s][:, hb * 8:hb * 8 + 4, :],
                             m1[s][:, hb * 8 + 4:hb * 8 + 8, :], ALU.add, [b_m1h[s][hb]], [b_mT[s]])
                    for hh in range(2):
                        for kc in range(8):
                            c.mm(pXo[:, hh * 512:(hh + 1) * 512], mT[s][:, kc, :], Wp[2][:, kc, hh * 512:(hh + 1) * 512],
                                 kc == 0, kc == 7, [b_mT[s], b_Wp[2]], [b_pXo])
                    c.tt("dve", xo[s][:], xt[s][:], pXo[:], ALU.add, [b_xt[s], b_pXo], [b_xo[s]])
                    c.dma("sp", xM[tk, :], xo[s][:], r=[b_xo[s]])

                p6_prep(0)
                for n in range(NT):
                    if n + 1 < NT:
                        p6_prep(n + 1)
                    p6_main(n)
                c.barrier()

            with c.open():
                if 'P7' not in phases:
                    raise _Stop()
                ident = c.sb("ident", [128, 128], BF16)
                identf = c.sb("identf", [128, 128], F32)
                b_id = c.buf("ident", dma=True)
                c.dma("sp", identf[:], k_id, w=[b_id])
                c.copy("dve", ident[:], identf[:], [b_id], [b_id])
                W1 = c.sb("W1", [128, 8, DFF], BF16)
                W2 = c.sb("W2", [128, 32, D], BF16)
                b_W1 = [c.buf("W1_%d" % i, dma=True) for i in range(4)]
                b_W2 = [c.buf("W2_%d" % i, dma=True) for i in range(4)]
                w1v = w_ff1[l].rearrange("(kc p) n -> p kc n", p=128)
                w2v = w_ff2[l].rearrange("(kc p) n -> p kc n", p=128)
                for i in range(4):
                    for j in range(2):
                        cs = slice(i * 1024 + j * 512, i * 1024 + (j + 1) * 512)
                        load_w(W1[:, :, cs], b_W1[i], w1v[:, :, cs])
                for i in range(4):
                    for j in range(2):
                        load_w(W2[:, i * 8:(i + 1) * 8, j * 512:(j + 1) * 512], b_W2[i],
                               w2v[:, i * 8:(i + 1) * 8, j * 512:(j + 1) * 512])
                g2 = c.sb("g2", [128, D], F32)
                b_g2 = c.buf("g2", dma=True)
                c.dma("sp", g2[:], norm2_g[l].partition_broadcast(128), w=[b_g2])
                if last:
                    gf = c.sb("gf", [128, D], F32)
                    b_gf = c.buf("gf", dma=True)
                    c.dma("sp", gf[:], final_g.partition_broadcast(128), w=[b_gf])

                def two(name, shape, dt, dma=False):
                    return ([c.sb("%s%d" % (name, i), shape, dt) for i in range(2)],
                            [c.buf("%s%d" % (name, i), dma=dma) for i in range(2)])
                xt, b_xt = two("xt7", [128, 2, D], F32, True)
                junk = c.sb("junk7", [128, D], BF16)
                b_junk = c.buf("junk7")
                ss, b_ss = two("ss7", [128, 2, 2], F32)
                hb = c.sb("hb7", [128, 2, D], BF16)
                b_hb = [c.buf("hb7_%d" % j) for j in range(2)]
                h2T, b_h2T = two("h2T", [128, 8, 256], BF16)
                rl, b_rl = two("rl", [128, 2, 256], F32)
                hid = c.sb("hid", [128, 32, 256], BF16)
                b_hid = [c.buf("hid%d" % i) for i in range(16)]
                xo, b_xo = two("xo7", [128, D], F32, True)
                fo = c.sb("fo7", [128, D], F32)
                b_fo = c.buf("fo7", dma=True)
                pT = c.ps("pT7", [128, 8, 128], BF16)
                pH = [c.ps("pH%d" % i, [128, 2, 256], F32) for i in range(4)]
                pO = c.ps("pO7", [128, D], F32)
                b_pT, b_pO = c.buf("pT7"), c.buf("pO7")
                b_pH = [c.buf("pH%d" % i) for i in range(4)]
                NPAIR = NT // 2

                def p7_prep(g):
                    s = g % 2
                    for j in range(2):
                        n = g * 2 + j
                        tk = slice(n * 128, (n + 1) * 128)
                        c.dma("sp", xt[s][:, j, :], xM[tk, :], w=[b_xt[s]])
                    for j in range(2):
                        ssj = ss[s][:, j, :]
                        c.act(junk[:], xt[s][:, j, :], AF.Square, [b_xt[s]], [b_junk, b_ss[s]], accum_out=ssj[:, 0:1])
                        rms_rstd(ssj, b_ss[s], D)
                        c.stt(hb[:, j, :], xt[s][:, j, :], ssj[:, 1:2], g2[:], ALU.mult, ALU.mult,
                              [b_xt[s], b_ss[s], b_g2], [b_hb[j]])
                        for kc in range(8):
                            c.tr(pT[:, kc, :], hb[:, j, kc * 128:(kc + 1) * 128], ident[:], [b_hb[j], b_id], [b_pT])
                        c.copy("dve", h2T[s][:, :, j * 128:(j + 1) * 128], pT[:], [b_pT], [b_h2T[s]])

                def p7_main(g):
                    s = g % 2
                    for hq in range(16):
                        p = hq % 4
                        for jj in range(2):
                            ht = hq * 2 + jj
                            for kc in range(8):
                                c.mm(pH[p][:, jj, :], W1[:, kc, ht * 128:(ht + 1) * 128], h2T[s][:, kc, :], kc == 0, kc == 7,
                                     [b_W1[ht // 8], b_h2T[s]], [b_pH[p]])
                        r2 = hq % 2
                        c.act(rl[r2][:], pH[p][:], AF.Relu, [b_pH[p]], [b_rl[r2]])
                        c.tt("pool" if hq % 2 else "dve", hid[:, hq * 2:(hq + 1) * 2, :], rl[r2][:], rl[r2][:], ALU.mult,
                             [b_rl[r2]], [b_hid[hq]])
                    for j in range(2):
                        n = g * 2 + j
                        tk = slice(n * 128, (n + 1) * 128)
                        xs = n % 2
                        for hh in range(2):
                            for kc in range(32):
                                c.mm(pO[:, hh * 512:(hh + 1) * 512], hid[:, kc, j * 128:(j + 1) * 128],
                                     W2[:, kc, hh * 512:(hh + 1) * 512],
                                     kc == 0, kc == 31, [b_hid[kc // 2], b_W2[kc // 8]], [b_pO])
                        c.tt("dve", xo[xs][:], xt[s][:, j, :], pO[:], ALU.add, [b_xt[s], b_pO], [b_xo[xs]])
                        if not last:
                            c.dma("sp", xA[tk, :], xo[xs][:], r=[b_xo[xs]])
                        else:
                            ssj = ss[s][:, j, :]
                            c.act(junk[:], xo[xs][:], AF.Square, [b_xo[xs]], [b_junk, b_ss[s]], accum_out=ssj[:, 0:1])
                            rms_rstd(ssj, b_ss[s], D)
                            c.stt(fo[:], xo[xs][:], ssj[:, 1:2], gf[:], ALU.mult, ALU.mult,
                                  [b_xo[xs], b_ss[s], b_gf], [b_fo])
                            c.dma("sp", out[tk, :], fo[:], r=[b_fo])

                p7_prep(0)
                for g in range(NPAIR):
                    if g + 1 < NPAIR:
                        p7_prep(g + 1)
                    p7_main(g)
                c.barrier()
          except _Stop:
            c.barrier()
            break
        n_inst = c.n_inst
    return nc, n_inst


def const_tables(S):
    f32 = np.float32
    half = 64
    inv = (f32(10000.0) ** (-np.arange(half, dtype=f32) / f32(half))).astype(f32)
    ang = (np.arange(S, dtype=f32)[:, None] * inv[None, :]).astype(f32)
    cos = np.cos(ang.astype(np.float64)).astype(f32).T
    sin = np.sin(ang.astype(np.float64)).astype(f32).T
    k_rot = np.stack([np.concatenate([cos, cos], 0), np.concatenate([-sin, sin], 0)], 0).astype(f32)
    idx = np.arange(128, dtype=np.float64)
    k_ret = np.zeros((5, 128, 4, 128), np.float64)
    sc = 128.0 ** -0.5
    for h in range(4):
        g = 1.0 - 2.0 ** (-5.0 - h)
        k_ret[0, :, h, :] = g ** np.abs(idx[:, None] - idx[None, :]) * sc
        k_ret[1, :, h, :] = (g ** (127.0 - idx))[:, None]
        k_ret[2, :, h, :] = (g ** idx)[:, None]
        k_ret[3, :, h, :] = (g ** (idx + 1.0))[None, :] * sc
        k_ret[4, :, h, :] = (g ** (128.0 - idx))[None, :] * sc
    k_ret = k_ret.reshape(5, 128, 512).astype(f32)
    a = np.arange(128)
    P, Fr = a[:, None], a[None, :]
    k_dn = np.stack([
        (P <= Fr), (P >= Fr),
        np.where(Fr <= P, 0.0, -BIG), np.where(Fr >= P, 0.0, -BIG),
        (Fr < P), (Fr > P)], 0).astype(f32)
    k_id = np.eye(128, dtype=f32)
    ms = []
    for lev in range(7):
        b = 1 << lev
        m = ((P // (2 * b) == Fr // (2 * b)) & ((P // b) % 2 == 1) & ((Fr // b) % 2 == 0))
        ms.append(m)
    k_dnm = np.stack(ms + [m.T for m in ms], 0).astype(f32)
    return {"k_rot": k_rot, "k_ret": k_ret, "k_dn": k_dn, "k_id": k_id, "k_dnm": k_dnm}


def make_in_maps(inputs, S, n_cores):
    depth = inputs["w_in"].shape[0]
    shared = {k: np.ascontiguousarray(v, dtype=np.float32) for k, v in inputs.items() if k != "x"}
    cwv = shared["conv_w"]
    shared["conv_w"] = np.ascontiguousarray(
        cwv.reshape(depth, 5, 24, 128).transpose(0, 3, 2, 1).reshape(depth, 128, 120))
    shared["a_log"] = shared["a_log"].reshape(depth, 16)
    shared["dt_bias"] = shared["dt_bias"].reshape(depth, 16)
    shared.update(const_tables(S))
    x = np.ascontiguousarray(inputs["x"], dtype=np.float32)
    maps = []
    for i in range(n_cores):
        m = dict(shared)
        m["x"] = x[i]
        maps.append(m)
    return maps


_PROG = {}


def kernel(**inputs):
    x = inputs["x"]
    B, S, _ = x.shape
    depth = inputs["w_in"].shape[0]
    key = (S, depth)
    if key not in _PROG:
        _PROG[key] = build_program(S, depth)[0]
    nc = _PROG[key]
    maps = make_in_maps(inputs, S, B)
    res = run_bass_kernel_spmd(nc, maps, core_ids=list(range(B)))
    return np.stack([np.asarray(r["out"], dtype=np.float32) for r in res.results], 0)
```
